# Optimizing a Trainium2 kernel written in Bass

```python
import jax, jax.numpy as jnp
from jax import lax
import numpy as np

D_MODEL = 1024
BATCH = 8
SEQ = 4096
DEPTH = 2

D_CONV = D_MODEL
D_SGU = D_MODEL
D_CFM = D_MODEL
N_SGU_GROUPS = 8
SGU_GROUP_DIM = D_SGU // N_SGU_GROUPS
CHUNK = 128
SHORT_K = 3
CFM_K = 31
N_BRANCH = 3
D_FF = -(-8 * D_MODEL // (3 * 256)) * 256
N_MOD = 6
EPS = 1e-6
SPLITS = [D_CONV, D_CONV, D_CONV, D_SGU, D_SGU, D_CFM, D_CFM, D_MODEL, D_MODEL, D_MODEL]
N_IN = sum(SPLITS)
SPLIT_IDX = list(np.cumsum(SPLITS)[:-1])

kernel_name = "hybrid_gated_conv_sgu_conformer_block"


def rms_norm(x, g):
    x32 = x.astype(jnp.float32)
    y = x32 * lax.rsqrt(jnp.mean(jnp.square(x32), axis=-1, keepdims=True) + EPS)
    return y.astype(x.dtype) * g


def layer_norm(x, g, b):
    x32 = x.astype(jnp.float32)
    mu = jnp.mean(x32, axis=-1, keepdims=True)
    var = jnp.mean(jnp.square(x32 - mu), axis=-1, keepdims=True)
    return ((x32 - mu) * lax.rsqrt(var + EPS)).astype(x.dtype) * g + b


def modulate(h, shift, scale):
    return h * (1 + scale[:, None, :]) + shift[:, None, :]


def causal_depthwise_conv(x, w):
    k, ch = w.shape
    return lax.conv_general_dilated(
        x, w[:, None, :].astype(x.dtype), window_strides=(1,), padding=[(k - 1, 0)],
        dimension_numbers=('NWC', 'WIO', 'NWC'), feature_group_count=ch)


def spatial_gating(u, v, ln_g, ln_b, w_s, b_s):
    bn, s, _ = v.shape
    v = layer_norm(v, ln_g, ln_b)
    v = v.reshape(bn, s // CHUNK, CHUNK, N_SGU_GROUPS, SGU_GROUP_DIM)
    mask = jnp.tril(jnp.ones((CHUNK, CHUNK), dtype=bool))
    w = jnp.where(mask[None], w_s, 0).astype(v.dtype)
    mixed = jnp.einsum('gts,bnsgc->bntgc', w, v) + b_s.T[None, None, :, :, None]
    return u * mixed.reshape(bn, s, D_SGU)


def setup_inputs(seed: int = 0) -> dict:
    key = jax.random.key(seed)
    ks = jax.random.split(key, 24)
    f32 = jnp.float32
    L, D = DEPTH, D_MODEL

    def nrm(k, shape, fan_in):
        return jax.random.normal(k, shape, f32) * (fan_in ** -0.5)

    def gain(k, shape):
        return 1.0 + 0.05 * jax.random.normal(k, shape, f32)

    def bias(k, shape):
        return 0.02 * jax.random.normal(k, shape, f32)

    return {
        "x": jax.random.normal(ks[0], (BATCH, SEQ, D), f32),
        "c": jax.random.normal(ks[1], (BATCH, D), f32),
        "w_ada": nrm(ks[2], (L, D, N_MOD * D), D),
        "b_ada": bias(ks[3], (L, N_MOD * D)),
        "norm1_g": gain(ks[4], (L, D)),
        "w_in": nrm(ks[5], (L, D, N_IN), D),
        "w_short": nrm(ks[6], (L, SHORT_K, D_CONV), SHORT_K),
        "w_a_out": nrm(ks[7], (L, D_CONV, D), D_CONV),
        "sgu_ln_g": gain(ks[8], (L, D_SGU)),
        "sgu_ln_b": bias(ks[9], (L, D_SGU)),
        "w_sgu": nrm(ks[10], (L, N_SGU_GROUPS, CHUNK, CHUNK), CHUNK),
        "b_sgu": 1.0 + 0.1 * jax.random.normal(ks[11], (L, N_SGU_GROUPS, CHUNK), f32),
        "w_b_out": nrm(ks[12], (L, D_SGU, D), D_SGU),
        "cfm_conv_w": nrm(ks[13], (L, CFM_K, D_CFM), CFM_K),
        "cfm_conv_b": bias(ks[14], (L, D_CFM)),
        "cfm_ln_g": gain(ks[15], (L, D_CFM)),
        "cfm_ln_b": bias(ks[16], (L, D_CFM)),
        "w_c_out": nrm(ks[17], (L, D_CFM, D), D_CFM),
        "w_o": nrm(ks[18], (L, D, D), D),
        "norm2_g": gain(ks[19], (L, D)),
        "w_ffn_in": nrm(ks[20], (L, D, 2 * D_FF), D),
        "w_ffn_out": nrm(ks[21], (L, D_FF, D), D_FF),
        "final_g": gain(ks[22], (D,)),
    }


def reference(x, c, w_ada, b_ada, norm1_g, w_in, w_short, w_a_out, sgu_ln_g, sgu_ln_b,
              w_sgu, b_sgu, w_b_out, cfm_conv_w, cfm_conv_b, cfm_ln_g, cfm_ln_b, w_c_out,
              w_o, norm2_g, w_ffn_in, w_ffn_out, final_g):
    bn, s, d = x.shape
    c_act = jax.nn.silu(c)
    for l in range(DEPTH):
        mod = c_act @ w_ada[l] + b_ada[l]
        shift1, scale1, gate1, shift2, scale2, gate2 = jnp.split(mod, N_MOD, axis=-1)

        h = modulate(rms_norm(x, norm1_g[l]), shift1, scale1)
        z = h @ w_in[l]
        b_a, c_a, x_a, u, v, a_c, g_c, gate_a, gate_b, gate_c = jnp.split(z, SPLIT_IDX, axis=-1)

        y_a = (b_a * causal_depthwise_conv(c_a * x_a, w_short[l])) @ w_a_out[l]

        y_b = spatial_gating(jax.nn.gelu(u), jax.nn.gelu(v), sgu_ln_g[l], sgu_ln_b[l],
                             w_sgu[l], b_sgu[l]) @ w_b_out[l]

        glu = a_c * jax.nn.sigmoid(g_c)
        conv = causal_depthwise_conv(glu, cfm_conv_w[l]) + cfm_conv_b[l]
        y_c = jax.nn.silu(layer_norm(conv, cfm_ln_g[l], cfm_ln_b[l])) @ w_c_out[l]

        merged = (jax.nn.sigmoid(gate_a) * y_a + jax.nn.sigmoid(gate_b) * y_b
                  + jax.nn.sigmoid(gate_c) * y_c)
        x = x + gate1[:, None, :] * (merged @ w_o[l])

        h2 = modulate(rms_norm(x, norm2_g[l]), shift2, scale2)
        g_f, u_f = jnp.split(h2 @ w_ffn_in[l], 2, axis=-1)
        x = x + gate2[:, None, :] * ((jax.nn.silu(g_f) * u_f) @ w_ffn_out[l])

    return rms_norm(x, final_g)
```

```python
import numpy as np
from contextlib import ExitStack
import concourse.bass as bass
import concourse.mybir as mybir
from concourse.bass_utils import run_bass_kernel_spmd

F32 = mybir.dt.float32
BF16 = mybir.dt.bfloat16
AF = mybir.ActivationFunctionType
ALU = mybir.AluOpType

D = 1024
KC = 8
DFF = 2816
FKC = 22
EPS = 1e-6
NCP = 4
NDV = 8 - NCP
NPB = 45
NPC = NPB + NCP
P_C, P_U, P_V, P_A, P_G, P_O, P_FI, P_FO = 0, 4, 6, 8, 14, 26, 28, 39
NADA = 12
CV_N1G, CV_N2G, CV_WSH, CV_CW, CV_CB, CV_CLG, CV_CLB, CV_BADA = 0, 8, 16, 40, 288, 296, 304, 312
NCV = 360
GH = 32


class _Eng:
    def __init__(self, name, be, sem, key, is_pe=False):
        self.name, self.be, self.sem, self.key, self.is_pe = name, be, sem, key, is_pe
        self.cnt = 0
        self.known = {}


class _DSem:
    def __init__(self, sem, key):
        self.sem, self.key, self.cnt = sem, key, 0


class _Res:
    __slots__ = ("name", "region", "lo", "hi", "w", "r")

    def __init__(self, name, region=None, lo=0, hi=0):
        self.name, self.region, self.lo, self.hi = name, region, lo, hi
        self.w = None
        self.r = {}
        if region is not None:
            region.append(self)

    def ov(self):
        if self.region is None:
            return (self,)
        return [o for o in self.region if o.lo < self.hi and self.lo < o.hi]


class _Prog:
    def __init__(self, nc, es):
        self.nc = nc
        self.es = es
        self.sems = {}
        self.nsem = 0
        self.pe = self._eng("pe", nc.tensor, True)
        self.act = self._eng("act", nc.scalar)
        self.dve = self._eng("dve", nc.vector)
        self.pool = self._eng("pool", nc.gpsimd)
        self.sp = _Eng("sp", nc.sync, None, -1)
        self.bgq = {"dve": [], "pool": [], "vn": []}
        self.log = None

    def _newsem(self, name):
        s = self.es.enter_context(self.nc.semaphore(name))
        k = self.nsem
        self.nsem += 1
        self.sems[k] = s
        return s, k

    def _eng(self, name, be, is_pe=False):
        s, k = self._newsem("e_" + name)
        return _Eng(name, be, s, k, is_pe)

    def dsem(self, name):
        s, k = self._newsem(name)
        return _DSem(s, k)

    def emit(self, eng, fn, reads=(), writes=(), inc=True, dsem=None):
        need = {}
        for r in reads:
            for o in r.ov():
                if o.w is not None and need.get(o.w[0], 0) < o.w[1]:
                    need[o.w[0]] = o.w[1]
        for w in writes:
            for o in w.ov():
                if o.w is not None and need.get(o.w[0], 0) < o.w[1]:
                    need[o.w[0]] = o.w[1]
                for k, v in o.r.items():
                    if need.get(k, 0) < v:
                        need[k] = v
        waits = []
        for k, v in need.items():
            if eng.is_pe and k == eng.key:
                continue
            if eng.known.get(k, 0) >= v:
                continue
            eng.known[k] = v
            waits.append((k, v))
        attach = None
        if waits and not eng.is_pe:
            attach = waits[-1]
        for k, v in waits:
            if (k, v) is not attach:
                eng.be.wait_ge(self.sems[k], v)
        ins = fn(eng.be)
        if attach is not None:
            ins._wait_ge(self.sems[attach[0]], attach[1])
        if dsem is not None:
            dsem.cnt += 16
            ins.then_inc(dsem.sem, 16)
            ev = (dsem.key, dsem.cnt)
        elif inc:
            eng.cnt += 1
            ins.then_inc(eng.sem, 1)
            ev = (eng.key, eng.cnt)
        else:
            ev = (eng.key, eng.cnt + 1)
        if self.log is not None:
            self.log.append((eng.name, waits, (ev[0], 16 if dsem is not None else (1 if inc else 0)), dsem is not None))
        for r in reads:
            if r.r.get(ev[0], 0) < ev[1]:
                r.r[ev[0]] = ev[1]
        for w in writes:
            w.w = ev
            w.r = {}
        return ev

    def pump(self, name, n):
        q = self.bgq[name]
        for _ in range(min(n, len(q))):
            q.pop(0)()

    def flush(self):
        for name in ("dve", "pool"):
            self.pump(name, len(self.bgq[name]))


def build_program(S, NL, T=512, NBUF=5, NTMP=10, PUMP=3, PUMP_A=1, PUMP_P=1, WARM=16):
    NT = S // T
    TC = T // 128
    nc = bass.Bass("TRN2", target_bir_lowering=False)
    dt = nc.dram_tensor
    x_d = dt("x", [128, KC, S], F32, kind="ExternalInput").ap()
    y_d = dt("y", [128, KC, S], F32, kind="ExternalOutput").ap()
    wpk_d = dt("wpk", [NL, NPC, 128, 4096], F32, kind="ExternalInput").ap()
    wada_d = dt("wada", [NL, NADA, 128, 4096], F32, kind="ExternalInput").ap()
    cv_d = dt("cv", [NL, 128, NCV], F32, kind="ExternalInput").ap()
    fg_d = dt("fg", [128, KC], F32, kind="ExternalInput").ap()
    cc_d = dt("cc", [128, KC], F32, kind="ExternalInput").ap()
    sln_d = dt("sln", [NL, 128, 2048], F32, kind="ExternalInput").ap()
    wsg_d = dt("wsg", [NL, 128, 1024], F32, kind="ExternalInput").ap()
    msk_d = dt("msk", [128, 1024], F32, kind="ExternalInput").ap()
    bsg_d = dt("bsg", [NL, 1, 1024], F32, kind="ExternalInput").ap()
    scr_d = dt("scr", [NL, NPC, 128, 4096], BF16, kind="Internal").ap()

    with ExitStack() as es:
        P = _Prog(nc, es)
        if _LOG is not None:
            P.log = _LOG
        pe, act, dve, pool, sp = P.pe, P.act, P.dve, P.pool, P.sp

        def sb(name, shape, dtype):
            return es.enter_context(nc.sbuf_tensor(name, shape, dtype))

        XB = [sb("xb%d" % i, [128, KC, T], F32) for i in range(2)]
        XREG = [[], []]
        XS_R = [[_Res("xs%d_%d" % (i, k), XREG[i], k * T * 4, (k + 1) * T * 4) for k in range(KC)] for i in range(2)]
        CA_R = [[_Res("ca%d_%d" % (i, k), XREG[i], k * T * 4, (k + 1) * T * 4) for k in range(KC)] for i in range(2)]
        YC_R = [[_Res("yc%d_%d" % (i, k), XREG[i], k * T * 4, k * T * 4 + T * 2) for k in range(KC)] for i in range(2)]
        xS2 = [P.dsem("xS%d" % i) for i in range(2)]
        hb = sb("hb", [128, KC, T], BF16)
        hR = [_Res("h%d" % k) for k in range(KC)]
        wring = [sb("wr%d" % i, [128, KC, 512], BF16) for i in range(NBUF)]
        wR = [_Res("wr%d" % i) for i in range(NBUF)]
        wld = [P.dsem("wld%d" % i) for i in range(NBUF)]
        wlds = [P.dsem("wlds%d" % i) for i in range(NBUF)]
        wst = [P.dsem("wst%d" % i) for i in range(NBUF)]
        gluf = sb("gluf", [128, max(NDV, 1), GH + T], F32)
        glub = sb("glub", [128, max(NCP, 1), GH + T], BF16)
        glR = [_Res("gl%d" % k) for k in range(KC)]
        G1 = []
        g1 = sb("g1b", [128, 12 * T], F32)
        gu = g1[:, 0:8 * T].rearrange("p (k t) -> p k t", k=KC)
        ya = g1[:, 8 * T:12 * T].bitcast(BF16).rearrange("p (k t) -> p k t", k=KC)
        ff = g1[:, 0:11 * T].bitcast(BF16).rearrange("p (k t) -> p k t", k=FKC)
        ost = g1[:, 0:8 * T].rearrange("p (k t) -> p k t", k=KC)
        guR = [_Res("gu%d" % k, G1, k * T * 4, (k + 1) * T * 4) for k in range(KC)]
        yaR = [_Res("ya%d" % k, G1, 8 * T * 4 + k * T * 2, 8 * T * 4 + (k + 1) * T * 2) for k in range(KC)]
        ffR = [_Res("ff%d" % k, G1, k * T * 2, (k + 1) * T * 2) for k in range(FKC)]
        osR = [_Res("os%d" % k, G1, k * T * 4, (k + 1) * T * 4) for k in range(KC)]
        g2 = sb("g2", [128, 8 * T], F32)
        G2 = []
        gv = g2[:, :].rearrange("p (c f) -> p c f", c=TC)
        mm_ = g2[:, :].rearrange("p (k t) -> p k t", k=KC)
        gvR = [_Res("gv%d" % c, G2, c * 4096, (c + 1) * 4096) for c in range(TC)]
        mR = [_Res("m%d" % k, G2, k * T * 4, (k + 1) * T * 4) for k in range(KC)]
        ost2 = g2[:, :].rearrange("p (k t) -> p k t", k=KC)
        osR2 = [_Res("os2_%d" % k, G2, k * T * 4, (k + 1) * T * 4) for k in range(KC)]
        pending_final = []
        g3 = sb("g3", [128, 8 * T], BF16)
        G3 = []
        nrm = g3[:, :].rearrange("p (c f) -> p c f", c=TC)
        mbf = g3[:, :].rearrange("p (k t) -> p k t", k=KC)
        nrR = [_Res("nr%d" % c, G3, c * 2048, (c + 1) * 2048) for c in range(TC)]
        mbR = [_Res("mb%d" % k, G3, k * T * 2, (k + 1) * T * 2) for k in range(KC)]
        yb = sb("yb", [128, KC, T], BF16)
        ybR = [_Res("yb%d" % k) for k in range(KC)]
        tmps = [sb("tmp%d" % i, [128, T + 8], F32) for i in range(NTMP)]
        tmpR = [_Res("tmp%d" % i) for i in range(NTMP)]
        tmpi = [0]

        def newtmp():
            i = tmpi[0] % NTMP
            tmpi[0] += 1
            return tmps[i], tmpR[i]

        rs = [sb("rs%d" % i, [128, T], F32) for i in range(2)]
        rsR = [_Res("rs%d" % i) for i in range(2)]
        mean_t = sb("mean_t", [128, T], F32)
        meanR = _Res("mean_t")
        cvt = sb("cvt", [128, NL, NCV], F32)
        cvR = _Res("cvt")
        fgt = sb("fgt", [128, KC], F32)
        cct = sb("cct", [128, KC], F32)
        cab = sb("cab", [128, KC], BF16)
        cabR = _Res("cab")
        modt = sb("modt", [128, NL, 48], F32)
        modR = [[_Res("mod%d_%d" % (l, j)) for j in range(NADA)] for l in range(NL)]
        At = sb("At", [128, NL, 16], F32)
        AR1 = [_Res("A1_%d" % l) for l in range(NL)]
        AR2 = [_Res("A2_%d" % l) for l in range(NL)]
        ones_f = sb("ones_f", [128, 128], F32)
        ones_b = sb("ones_b", [128, 128], BF16)
        epst = sb("epst", [128, 1], F32)
        ones_m = sb("ones_m", [128, 128], BF16)
        onesR = _Res("ones")
        wsgt = g2[:, 0:1024]
        mskt = g2[:, 1024:2048]
        wsR = _Res("wsgt", G2, 0, 4096)
        mskR = _Res("mskt", G2, 4096, 8192)
        wmt = sb("wmt", [128, NL, 1024], BF16)
        wmR = _Res("wmt")
        bsf = g2[0:1, 2048:2048 + NL * 1024].rearrange("o (l n) -> o l n", l=NL)
        bsfR = _Res("bsf", G2, 8192, 8192 + NL * 4096)
        bsh = sb("bsh", [1, NL, 1024], BF16)
        bsl = sb("bsl", [1, NL, 1024], BF16)
        bsR = _Res("bs")
        slnt = sb("slnt", [128, 2048], F32)
        slnR = _Res("sln")
        slnS = P.dsem("slnS")
        gst = sb("gst", [128, NL, max(NDV, 1), GH], F32)
        gstb = sb("gstb", [128, NL, max(NCP, 1), GH], BF16)
        gstR = [_Res("gst%d" % l) for l in range(NL)]
        pst = sb("pst", [128, NL, KC, 2], F32)
        pstR = [_Res("pst%d" % l) for l in range(NL)]
        bst = sb("bst", [128, TC, 2, 6], F32)
        bag = sb("bag", [128, TC, 2], F32)
        brs = sb("brs", [128, TC], F32)
        bnR = [_Res("bn%d" % c) for c in range(TC)]
        brsR = _Res("brs")
        constR = _Res("const")
        oS = P.dsem("oS")
        cS = P.dsem("cS")
        scrR = [[_Res("scr%d_%d" % (l, q)) for q in range(NPC)] for l in range(NL)]

        ps = es.enter_context(nc.psum_tensor("ps", [128, 8, 512], F32))
        bkR = [_Res("bank%d" % i) for i in range(8)]
        bki = [0]

        def nb():
            i = bki[0] % 8
            bki[0] += 1
            return i

        cdmas = [
            (cvt[:, :, :], cv_d.rearrange("l p n -> p l n")),
            (fgt[:, :], fg_d), (cct[:, :], cc_d), (mskt, msk_d),
            (bsf, bsg_d.rearrange("l o n -> o l n")),
        ]
        for o, i in cdmas:
            P.emit(sp, lambda e, o=o, i=i: e.dma_start(out=o, in_=i), dsem=cS)
        constR.w = (cS.key, cS.cnt)
        cvR.w = constR.w
        mskR.w = constR.w
        bsfR.w = constR.w
        P.emit(pool, lambda e: e.memset(ones_f[:, :], 1.0 / D), writes=[onesR])
        P.emit(pool, lambda e: e.memset(ones_b[:, :], 1.0), writes=[onesR])
        P.emit(pool, lambda e: e.memset(epst[:, :], EPS), writes=[onesR])
        P.emit(pool, lambda e: e.memset(ones_m[:, :], 1.0 / D), writes=[onesR])
        for l in range(NL):
            P.emit(pool, lambda e, l=l: e.memset(gst[:, l, :, :], 0.0), writes=[gstR[l]])
            P.emit(pool, lambda e, l=l: e.memset(gstb[:, l, :, :], 0.0), writes=[gstR[l]])
            P.emit(pool, lambda e, l=l: e.memset(pst[:, l, :, :], 0.0), writes=[pstR[l]])
        P.emit(act, lambda e: e.activation(out=cab[:, :], in_=cct[:, :], func=AF.Silu), reads=[constR], writes=[cabR])
        wsS = P.dsem("wsS")
        for l in range(NL):
            P.emit(sp, lambda e, l=l: e.dma_start(out=wsgt, in_=wsg_d[l]), writes=[wsR], dsem=wsS)
            P.emit(dve, lambda e, l=l: e.tensor_tensor(out=wmt[:, l, :], in0=wsgt, in1=mskt, op=ALU.mult),
                   reads=[wsR, mskR], writes=[wmR])
        P.emit(act, lambda e: e.activation(out=bsh[:, :, :], in_=bsf, func=AF.Copy), reads=[bsfR], writes=[bsR])
        P.emit(dve, lambda e: e.tensor_tensor(out=bsl[:, :, :], in0=bsf, in1=bsh[:, :, :], op=ALU.subtract),
               reads=[bsfR, bsR], writes=[bsR])

        def ada_sched(l, tag):
            if tag == "start" and l == 0:
                return [(0, j) for j in range(4)]
            if tag == "after_C" and l == 0:
                return [(0, 4), (0, 5)]
            if tag == "after_V" and l == 0:
                return [(0, j) for j in range(6, 10)]
            if tag == "after_G" and l == 0:
                return [(0, 10), (0, 11)]
            if l + 1 < NL:
                if tag.startswith("ffn_in_"):
                    return [(l + 1, int(tag[7:]))]
                if tag == "ffn_out_0":
                    return [(l + 1, 11)]
            return []

        def layer_seq(it, l):
            seq = []

            def ada(tag):
                if it == 0:
                    seq.extend(("ada", ll, j, it) for (ll, j) in ada_sched(l, tag))

            def w(qs):
                seq.extend(("w", l, q, it) for q in qs)
            ada("start")
            w(range(P_C, P_U))
            ada("after_C")
            w(range(P_U, P_V))
            w(range(P_V, P_A))
            w(range(NPB, NPC))
            ada("after_V")
            w(range(P_A, P_G))
            w(range(P_G, P_O))
            ada("after_G")
            w(range(P_O, P_FI))
            for i in range(11):
                w([P_FI + i])
                ada("ffn_in_%d" % i)
            for i in range(6):
                w([P_FO + i])
                if i == 0:
                    ada("ffn_out_0")
            return seq

        stream = []
        for it in range(NT):
            for l in range(NL):
                stream.extend(layer_seq(it, l))
        nload = [0]

        def issue_load():
            i = nload[0]
            if i >= len(stream):
                return
            nload[0] += 1
            kind, l, q, it = stream[i]
            s = i % NBUF
            dst = wring[s][:, :, :].rearrange("p k c -> p (k c)")
            if kind == "ada":
                P.emit(pool, lambda e: e.dma_start(out=dst, in_=wada_d[l, q]), writes=[wR[s]], dsem=wlds[s])
            elif it == 0:
                P.emit(pool, lambda e: e.dma_start(out=dst, in_=wpk_d[l, q]), writes=[wR[s]], dsem=wlds[s])
                P.emit(sp, lambda e: e.dma_start(out=scr_d[l, q], in_=dst), reads=[wR[s]], writes=[scrR[l][q]], dsem=wst[s])
            else:
                P.emit(sp, lambda e: e.dma_start(out=dst, in_=scr_d[l, q]), reads=[scrR[l][q]], writes=[wR[s]], dsem=wld[s])

        nuse = [0]

        def next_piece(kind, l, q):
            i = nuse[0]
            nuse[0] += 1
            assert stream[i][0] == kind and stream[i][1] == l and stream[i][2] == q, (stream[i], kind, l, q)
            issue_load()
            s = i % NBUF
            return wring[s], wR[s]

        for _ in range(NBUF - 2):
            issue_load()

        def group(bank, out_ap, items, inc_all=False):
            n = len(items)
            for i, (lt, rh, rd) in enumerate(items):
                last = i == n - 1
                P.emit(pe, lambda e, lt=lt, rh=rh, i=i, last=last: e.matmul(out_ap, lt, rh, start=(i == 0), stop=last),
                       reads=rd, writes=[bkR[bank]] if i == 0 else [], inc=(last or inc_all))

        def groups_ilv(specs):
            n = len(specs[0][2])
            for i in range(n):
                for (bank, out_ap, items) in specs:
                    lt, rh, rd = items[i]
                    last = i == n - 1
                    P.emit(pe, lambda e, lt=lt, rh=rh, out_ap=out_ap, last=last: e.matmul(out_ap, lt, rh, start=(i == 0), stop=last),
                           reads=rd, writes=[bkR[bank]] if i == 0 else [], inc=last)
                    if last:
                        bkR[bank].w = (pe.key, pe.cnt)

        def groups_seq(specs):
            for (bank, out_ap, items) in specs:
                group(bank, out_ap, items)
                group_done(bank)

        def group_done(bank):
            bkR[bank].w = (pe.key, pe.cnt)

        def after_group():
            P.pump("dve", PUMP)

        def warm(n):
            if n <= 0:
                return
            b = nb()
            for i in range(n):
                P.emit(pe, lambda e: e.matmul(ps[:, b, :], ones_m[:, :], wmt[:, 0, 0:512], start=True, stop=True),
                       reads=[onesR, wmR], writes=[bkR[b]] if i == 0 else [], inc=False)

        def rmsnorm_to(l, A_ap, sh_ap, dst, dstR, xreads=(), xs_=None, xsR_=None, ri=0):
            xs_ = xs if xs_ is None else xs_
            xsR_ = xsR if xsR_ is None else xsR_
            b = nb()
            for kc in range(KC):
                t, tR = newtmp()
                tb = t[:, 0:T // 2].bitcast(BF16)
                P.emit(act, lambda e, tb=tb, kc=kc: e.activation(out=tb, in_=xs_[:, kc, :], func=AF.Square),
                       reads=[xsR_[kc]], writes=[tR])
                P.emit(pe, lambda e, tb=tb, kc=kc: e.matmul(ps[:, b, :], ones_m[:, :], tb, start=(kc == 0), stop=(kc == KC - 1)),
                       reads=[tR, onesR], writes=[bkR[b]] if kc == 0 else [], inc=True)
            group_done(b)
            if sh_ap is not None:
                warm(WARM)
            r, rR = rs[ri], rsR[ri]
            P.emit(act, lambda e: e.activation(out=r[:, :], in_=ps[:, b, :], func=AF.Ln, bias=epst[:, 0:1], scale=1.0), reads=[bkR[b], onesR], writes=[rR])
            P.emit(act, lambda e: e.activation(out=r[:, :], in_=r[:, :], func=AF.Exp, scale=-0.5), reads=[rR], writes=[rR])
            for kc in range(KC):
                t, tR = newtmp()
                eng = dve
                P.emit(eng, lambda e, t=t, kc=kc: e.tensor_tensor(out=t[:, 0:T], in0=xs_[:, kc, :], in1=r[:, :], op=ALU.mult),
                       reads=[xsR_[kc], rR], writes=[tR])
                if sh_ap is not None:
                    P.emit(act, lambda e, t=t, kc=kc: e.activation(out=dst[:, kc, :], in_=t[:, 0:T], func=AF.Identity,
                                                                   scale=A_ap[:, kc:kc + 1], bias=sh_ap[:, kc:kc + 1]),
                           reads=[tR] + list(xreads), writes=[dstR[kc]])
                else:
                    P.emit(act, lambda e, t=t, kc=kc: e.activation(out=dst[:, kc, :], in_=t[:, 0:T], func=AF.Identity,
                                                                   scale=A_ap[:, kc:kc + 1], bias=0.0),
                           reads=[tR, constR], writes=[dstR[kc]])

        def cvcol(l, base, k):
            return cvt[:, l, base + k:base + k + 1]

        P.emit(sp, lambda e: e.dma_start(out=XB[0][:, :, :], in_=x_d[:, :, 0:T]), writes=XS_R[0], dsem=xS2[0])
        for it in range(NT):
            t0 = it * T
            cur = it % 2
            xs, xsR = XB[cur], XS_R[cur]
            cacc, caR = XB[1 - cur], CA_R[1 - cur]
            yc = cacc[:, :, 0:T // 2].bitcast(BF16)
            ycR = YC_R[1 - cur]
            for l in range(NL):
                P.emit(sp, lambda e: e.dma_start(out=slnt[:, :], in_=sln_d[l]), writes=[slnR], dsem=slnS)
                def ada_hook(tag, l=l, it=it):
                    if it != 0:
                        return
                    for (lt, j) in ada_sched(l, tag):
                        wt, wr_ = next_piece("ada", lt, j)
                        b = nb()
                        for c4 in range(4):
                            for kc in range(KC):
                                P.emit(pe, lambda e, wt=wt, c4=c4, kc=kc: e.matmul(
                                    ps[:, b, c4:c4 + 1], wt[:, kc, c4 * 128:(c4 + 1) * 128], cab[:, kc:kc + 1],
                                    start=(kc == 0), stop=(kc == KC - 1)),
                                    reads=[wr_, cabR], writes=[bkR[b]] if (c4 == 0 and kc == 0) else [], inc=(kc == KC - 1))
                        group_done(b)
                        P.emit(dve, lambda e, lt=lt, j=j, b=b: e.tensor_tensor(out=modt[:, lt, 4 * j:4 * j + 4], in0=ps[:, b, 0:4],
                                                                              in1=cvt[:, lt, CV_BADA + 4 * j:CV_BADA + 4 * j + 4], op=ALU.add),
                               reads=[bkR[b], cvR], writes=[modR[lt][j]])
                        if j == 3:
                            P.emit(dve, lambda e, lt=lt: e.scalar_tensor_tensor(out=At[:, lt, 0:8], in0=modt[:, lt, 8:16], scalar=1.0, in1=cvt[:, lt, CV_N1G:CV_N1G + 8],
                                                                               op0=ALU.add, op1=ALU.mult), reads=[modR[lt][2], modR[lt][3], cvR], writes=[AR1[lt]])
                        if j == 9:
                            P.emit(dve, lambda e, lt=lt: e.scalar_tensor_tensor(out=At[:, lt, 8:16], in0=modt[:, lt, 32:40], scalar=1.0, in1=cvt[:, lt, CV_N2G:CV_N2G + 8],
                                                                               op0=ALU.add, op1=ALU.mult), reads=[modR[lt][8], modR[lt][9], cvR], writes=[AR2[lt]])

                ada_hook("start")
                sh1 = modt[:, l, 0:8]
                g1v = modt[:, l, 16:24]
                sh2 = modt[:, l, 24:32]
                g2v = modt[:, l, 40:48]

                if not (it > 0 and l == 0):
                    rmsnorm_to(l, At[:, l, 0:8], sh1, hb, hR, xreads=[AR1[l], modR[l][0], modR[l][1]])

                if NDV:
                    P.emit(pool, lambda e: e.tensor_copy(out=gluf[:, :, 0:GH], in_=gst[:, l, :, :]), reads=[gstR[l]], writes=glR[:NDV])
                if NCP:
                    P.emit(pool, lambda e: e.tensor_copy(out=glub[:, :, 0:GH], in_=gstb[:, l, :, :]), reads=[gstR[l]], writes=glR[NDV:])

                o0 = GH - 30
                def c_evac(oc, ba, bg, l=l):
                        t, tR = newtmp()
                        P.emit(act, lambda e, t=t, bg=bg: e.activation(out=t[:, 0:T], in_=ps[:, bg, :], func=AF.Sigmoid), reads=[bkR[bg]], writes=[tR])
                        if oc >= NDV:
                            P.emit(dve, lambda e, t=t, ba=ba, oc=oc: e.tensor_tensor(out=glub[:, oc - NDV, GH:GH + T], in0=ps[:, ba, :], in1=t[:, 0:T], op=ALU.mult),
                                   reads=[bkR[ba], tR], writes=[glR[oc]])
                            return
                        P.emit(dve, lambda e, t=t, ba=ba, oc=oc: e.tensor_tensor(out=gluf[:, oc, GH:GH + T], in0=ps[:, ba, :], in1=t[:, 0:T], op=ALU.mult),
                               reads=[bkR[ba], tR], writes=[glR[oc]])
                        ename = "dve"
                        eng = dve

                        def tap_first(eng=eng, oc=oc, l=l):
                            P.emit(act, lambda e: e.activation(out=cacc[:, oc, :], in_=gluf[:, oc, o0 + 30:o0 + 30 + T], func=AF.Identity,
                                                               scale=cvcol(l, CV_CW, 30 * 8 + oc), bias=cvcol(l, CV_CB, oc)),
                                   reads=[glR[oc], cvR], writes=[caR[oc]])

                        def tap(k, eng=eng, oc=oc, l=l):
                            P.emit(eng, lambda e: e.scalar_tensor_tensor(out=cacc[:, oc, :], in0=gluf[:, oc, o0 + k:o0 + k + T],
                                                                         scalar=cvcol(l, CV_CW, k * 8 + oc), in1=cacc[:, oc, :],
                                                                         op0=ALU.mult, op1=ALU.add),
                                   reads=[glR[oc], cvR, caR[oc]], writes=[caR[oc]])
                        P.bgq[ename].append(tap_first)
                        for k in range(29, -1, -1):
                            P.bgq[ename].append(lambda k=k, tap=tap: tap(k))

                for pr in range(4):
                    wt, wr_ = next_piece("w", l, P_C + pr)
                    specs, cb = [], []
                    for oc in (2 * pr, 2 * pr + 1):
                        s0 = (oc % 2) * 2
                        ba, bg = nb(), nb()
                        cb.append((oc, ba, bg))
                        specs.append((ba, ps[:, ba, :], [(wt[:, kc, s0 * 128:(s0 + 1) * 128], hb[:, kc, :], [wr_, hR[kc]]) for kc in range(KC)]))
                        specs.append((bg, ps[:, bg, :], [(wt[:, kc, (s0 + 1) * 128:(s0 + 2) * 128], hb[:, kc, :], [wr_, hR[kc]]) for kc in range(KC)]))
                    (groups_ilv if pr == 0 else groups_seq)(specs)
                    for (oc, ba, bg) in cb:
                        c_evac(oc, ba, bg)
                ada_hook("after_C")
                while l == 0 and pending_final:
                    pending_final.pop(0)()
                if NDV:
                    P.emit(pool, lambda e: e.tensor_copy(out=gst[:, l, :, :], in_=gluf[:, :, T:T + GH]), reads=glR[:NDV], writes=[gstR[l]])
                if NCP:
                    P.emit(pool, lambda e: e.tensor_copy(out=gstb[:, l, :, :], in_=glub[:, :, T:T + GH]), reads=glR[NDV:], writes=[gstR[l]])

                def pe_conv():
                    for i in range(NCP):
                        oc = NDV + i
                        wt, wr_ = next_piece("w", l, NPB + i)
                        b = nb()
                        group(b, ps[:, b, :], [(wt[:, k // 4, (k % 4) * 128:(k % 4 + 1) * 128], glub[:, i, o0 + k:o0 + k + T], [wr_, glR[oc]]) for k in range(31)])
                        group_done(b)
                        P.emit(act, lambda e, b=b, oc=oc: e.activation(out=cacc[:, oc, :], in_=ps[:, b, :], func=AF.Identity,
                                                                       bias=cvcol(l, CV_CB, oc), scale=1.0), reads=[bkR[b], cvR], writes=[caR[oc]])
                        after_group()
                        P.pump("vn", 2)

                for oc in range(KC):
                    if oc % 4 == 0:
                        wt, wr_ = next_piece("w", l, P_U + oc // 4)
                    s0 = oc % 4
                    b = nb()
                    group(b, ps[:, b, :], [(wt[:, kc, s0 * 128:(s0 + 1) * 128], hb[:, kc, :], [wr_, hR[kc]]) for kc in range(KC)])
                    group_done(b)
                    P.emit(act, lambda e, b=b, oc=oc: e.activation(out=gu[:, oc, :], in_=ps[:, b, :], func=AF.Gelu_apprx_tanh),
                           reads=[bkR[b]], writes=[guR[oc]])
                    after_group()

                wv = [next_piece("w", l, P_V), next_piece("w", l, P_V + 1)]
                for tc in range(TC):
                    for half in range(2):
                        wt, wr_ = wv[half]
                        b = nb()
                        group(b, ps[:, b, :], [(hb[:, kc, tc * 128:(tc + 1) * 128], wt[:, kc, :], [wr_, hR[kc]]) for kc in range(KC)])
                        group_done(b)
                        P.emit(act, lambda e, b=b, tc=tc, half=half: e.activation(out=gv[:, tc, half * 512:(half + 1) * 512], in_=ps[:, b, :],
                                                                                  func=AF.Gelu_apprx_tanh),
                               reads=[bkR[b]], writes=[gvR[tc]])
                        after_group()
                    for half in range(2):
                        P.emit(dve, lambda e, tc=tc, half=half: e.bn_stats(out=bst[:, tc, half, :], in_=gv[:, tc, half * 512:(half + 1) * 512]),
                               reads=[gvR[tc]], writes=[bnR[tc]])
                    P.emit(dve, lambda e, tc=tc: e.bn_aggr(out=bag[:, tc, :], in_=bst[:, tc, :, :]), reads=[bnR[tc]], writes=[bnR[tc]])
                P.emit(act, lambda e: e.activation(out=brs[:, :], in_=bag[:, :, 1], func=AF.Ln, bias=epst[:, 0:1], scale=1.0), reads=bnR + [onesR], writes=[brsR])
                P.emit(act, lambda e: e.activation(out=brs[:, :], in_=brs[:, :], func=AF.Exp, scale=-0.5), reads=[brsR], writes=[brsR])
                for tc in range(TC):
                    def vn1(tc=tc):
                        P.emit(dve, lambda e: e.scalar_tensor_tensor(out=gv[:, tc, :], in0=gv[:, tc, :], scalar=bag[:, tc, 0:1], in1=slnt[:, 0:1024],
                                                                     op0=ALU.subtract, op1=ALU.mult),
                               reads=[gvR[tc], bnR[tc], slnR], writes=[gvR[tc]])

                    def vn2(tc=tc):
                        P.emit(dve, lambda e: e.scalar_tensor_tensor(out=nrm[:, tc, :], in0=gv[:, tc, :], scalar=brs[:, tc:tc + 1], in1=slnt[:, 1024:2048],
                                                                     op0=ALU.mult, op1=ALU.add),
                               reads=[gvR[tc], brsR, slnR], writes=[nrR[tc]])
                    P.bgq["vn"].append(vn1)
                    P.bgq["vn"].append(vn2)

                pe_conv()
                ada_hook("after_V")

                P_aw = [None]
                for oc in range(KC):
                    banks = []
                    for j in range(3):
                        sA = oc * 3 + j
                        if sA % 4 == 0:
                            P_aw[0] = next_piece("w", l, P_A + sA // 4)
                        wt, wr_ = P_aw[0]
                        s0 = sA % 4
                        b = nb()
                        group(b, ps[:, b, :], [(wt[:, kc, s0 * 128:(s0 + 1) * 128], hb[:, kc, :], [wr_, hR[kc]]) for kc in range(KC)])
                        group_done(b)
                        banks.append(b)
                    b0, b1, b2 = banks
                    ca_, caR_ = newtmp()
                    P.emit(act, lambda e, ca_=ca_, b0=b0: e.activation(out=ca_[:, 0:T], in_=ps[:, b0, :], func=AF.Copy), reads=[bkR[b0]], writes=[caR_])
                    pt, ptR = newtmp()
                    P.emit(pool, lambda e, pt=pt, oc=oc: e.tensor_copy(out=pt[:, 0:2], in_=pst[:, l, oc, :]), reads=[pstR[l]], writes=[ptR])
                    P.emit(dve, lambda e, pt=pt, ca_=ca_, b1=b1: e.tensor_tensor(out=pt[:, 2:2 + T], in0=ps[:, b1, :], in1=ca_[:, 0:T], op=ALU.mult),
                           reads=[bkR[b1], caR_], writes=[ptR])
                    P.emit(pool, lambda e, pt=pt, oc=oc: e.tensor_copy(out=pst[:, l, oc, :], in_=pt[:, T:T + 2]), reads=[ptR], writes=[pstR[l]])
                    qt, qtR = newtmp()
                    P.emit(act, lambda e, pt=pt, qt=qt, oc=oc: e.activation(out=qt[:, 0:T], in_=pt[:, 2:2 + T], func=AF.Identity,
                                                                           scale=cvcol(l, CV_WSH, 2 * 8 + oc), bias=0.0), reads=[ptR, cvR], writes=[qtR])
                    for k in (1, 0):
                        P.emit(dve, lambda e, pt=pt, qt=qt, oc=oc, k=k: e.scalar_tensor_tensor(out=qt[:, 0:T], in0=pt[:, k:k + T], scalar=cvcol(l, CV_WSH, k * 8 + oc),
                                                                                              in1=qt[:, 0:T], op0=ALU.mult, op1=ALU.add),
                               reads=[ptR, cvR, qtR], writes=[qtR])
                    P.emit(dve, lambda e, qt=qt, b2=b2, oc=oc: e.tensor_tensor(out=ya[:, oc, :], in0=ps[:, b2, :], in1=qt[:, 0:T], op=ALU.mult),
                           reads=[bkR[b2], qtR], writes=[yaR[oc]])
                    P.pump("dve", PUMP_A)
                    P.pump("vn", 1)

                P.pump("vn", len(P.bgq["vn"]))
                for g in range(KC):
                    b = nb()
                    for tc in range(TC):
                        o_ap = ps[:, b, tc * 128:(tc + 1) * 128]
                        first = tc == 0
                        P.emit(pe, lambda e, o_ap=o_ap, tc=tc, g=g: e.matmul(o_ap, nrm[:, tc, g * 128:(g + 1) * 128], wmt[:, l, g * 128:(g + 1) * 128],
                                                                             start=True, stop=False),
                               reads=[nrR[tc], wmR], writes=[bkR[b]] if first else [], inc=False)
                        P.emit(pe, lambda e, o_ap=o_ap, g=g: e.matmul(o_ap, ones_b[0:1, :], bsh[0:1, l, g * 128:(g + 1) * 128], start=False, stop=False),
                               reads=[onesR, bsR], inc=False)
                        P.emit(pe, lambda e, o_ap=o_ap, g=g: e.matmul(o_ap, ones_b[0:1, :], bsl[0:1, l, g * 128:(g + 1) * 128], start=False, stop=True),
                               reads=[onesR, bsR], inc=(tc == TC - 1))
                    group_done(b)
                    P.emit(dve, lambda e, b=b, g=g: e.tensor_tensor(out=yb[:, g, :], in0=ps[:, b, :], in1=gu[:, g, :], op=ALU.mult),
                           reads=[bkR[b], guR[g]], writes=[ybR[g]])
                    after_group()

                def conformer_ln():
                    P.flush()
                    b1, b2 = nb(), nb()
                    for kc in range(KC):
                        t, tR = newtmp()
                        tb1 = t[:, 0:T // 2].bitcast(BF16)
                        tb2 = t[:, T // 2:T].bitcast(BF16)
                        P.emit(act, lambda e, tb1=tb1, kc=kc: e.activation(out=tb1, in_=cacc[:, kc, :], func=AF.Copy), reads=[caR[kc]], writes=[tR])
                        P.emit(act, lambda e, tb2=tb2, kc=kc: e.activation(out=tb2, in_=cacc[:, kc, :], func=AF.Square), reads=[caR[kc]], writes=[tR])
                        P.emit(pe, lambda e, tb1=tb1, kc=kc: e.matmul(ps[:, b1, :], ones_m[:, :], tb1, start=(kc == 0), stop=(kc == KC - 1)),
                               reads=[tR, onesR], writes=[bkR[b1]] if kc == 0 else [], inc=True)
                        P.emit(pe, lambda e, tb2=tb2, kc=kc: e.matmul(ps[:, b2, :], ones_m[:, :], tb2, start=(kc == 0), stop=(kc == KC - 1)),
                               reads=[tR, onesR], writes=[bkR[b2]] if kc == 0 else [], inc=True)
                    bkR[b1].w = (pe.key, pe.cnt - 1)
                    group_done(b2)
                    warm(WARM)
                    P.emit(act, lambda e: e.activation(out=mean_t[:, :], in_=ps[:, b1, :], func=AF.Copy), reads=[bkR[b1]], writes=[meanR])
                    t, tR = newtmp()
                    P.emit(dve, lambda e, t=t: e.tensor_tensor(out=t[:, 0:T], in0=mean_t[:, :], in1=mean_t[:, :], op=ALU.mult), reads=[meanR], writes=[tR])
                    P.emit(dve, lambda e, t=t: e.tensor_tensor(out=t[:, 0:T], in0=ps[:, b2, :], in1=t[:, 0:T], op=ALU.subtract), reads=[bkR[b2], tR], writes=[tR])
                    rC, rCR = rs[1], rsR[1]
                    P.emit(act, lambda e, t=t: e.activation(out=rC[:, :], in_=t[:, 0:T], func=AF.Ln, bias=epst[:, 0:1], scale=1.0), reads=[tR, onesR], writes=[rCR])
                    P.emit(act, lambda e: e.activation(out=rC[:, :], in_=rC[:, :], func=AF.Exp, scale=-0.5), reads=[rCR], writes=[rCR])
                    for kc in range(KC):
                        t, tR = newtmp()
                        eng = dve
                        P.emit(eng, lambda e, t=t, kc=kc: e.tensor_tensor(out=t[:, 0:T], in0=cacc[:, kc, :], in1=mean_t[:, :], op=ALU.subtract),
                               reads=[caR[kc], meanR], writes=[tR])
                        P.emit(eng, lambda e, t=t: e.tensor_tensor(out=t[:, 0:T], in0=t[:, 0:T], in1=rC[:, :], op=ALU.mult), reads=[tR, rCR], writes=[tR])
                        P.emit(act, lambda e, t=t, kc=kc: e.activation(out=yc[:, kc, :], in_=t[:, 0:T], func=AF.Silu,
                                                                       scale=cvcol(l, CV_CLG, kc), bias=cvcol(l, CV_CLB, kc)),
                               reads=[tR, cvR], writes=[ycR[kc]])

                for X, (yX, yXR) in enumerate(((ya, yaR), (yb, ybR), (yc, ycR))):
                    if X == 2:
                        conformer_ln()
                    for half in range(2):
                        wg, wgR = next_piece("w", l, P_G + X * 4 + half * 2)
                        wo, woR = next_piece("w", l, P_G + X * 4 + half * 2 + 1)
                        pre = {}
                        if X == 2 and half == 0:
                            for j in range(4):
                                bg_ = nb()
                                group(bg_, ps[:, bg_, :], [(wg[:, kc, j * 128:(j + 1) * 128], hb[:, kc, :], [wgR, hR[kc]]) for kc in range(KC)])
                                group_done(bg_)
                                sg, sgR = newtmp()
                                P.emit(act, lambda e, sg=sg, bg_=bg_: e.activation(out=sg[:, 0:T], in_=ps[:, bg_, :], func=AF.Sigmoid), reads=[bkR[bg_]], writes=[sgR])
                                pre[j] = (sg, sgR)
                        ybk = {}
                        if pre:
                            specs = []
                            for j in range(4):
                                ybk[j] = nb()
                                specs.append((ybk[j], ps[:, ybk[j], :], [(wo[:, kc, j * 128:(j + 1) * 128], yX[:, kc, :], [woR, yXR[kc]]) for kc in range(KC)]))
                            groups_ilv(specs)
                        for j in range(4):
                            oc = half * 4 + j
                            if j in pre:
                                sg, sgR = pre[j]
                                by_ = ybk[j]
                            else:
                                bg_, by_ = nb(), nb()
                                group(bg_, ps[:, bg_, :], [(wg[:, kc, j * 128:(j + 1) * 128], hb[:, kc, :], [wgR, hR[kc]]) for kc in range(KC)])
                                group_done(bg_)
                            if j not in pre:
                                group(by_, ps[:, by_, :], [(wo[:, kc, j * 128:(j + 1) * 128], yX[:, kc, :], [woR, yXR[kc]]) for kc in range(KC)])
                                group_done(by_)
                            if j not in pre:
                                sg, sgR = newtmp()
                                P.emit(act, lambda e, sg=sg, bg_=bg_: e.activation(out=sg[:, 0:T], in_=ps[:, bg_, :], func=AF.Sigmoid), reads=[bkR[bg_]], writes=[sgR])
                            if X == 0:
                                P.emit(dve, lambda e, sg=sg, by_=by_, oc=oc: e.tensor_tensor(out=mm_[:, oc, :], in0=ps[:, by_, :], in1=sg[:, 0:T], op=ALU.mult),
                                       reads=[bkR[by_], sgR], writes=[mR[oc]])
                            else:
                                P.emit(dve, lambda e, sg=sg, by_=by_: e.tensor_tensor(out=sg[:, 0:T], in0=ps[:, by_, :], in1=sg[:, 0:T], op=ALU.mult),
                                       reads=[bkR[by_], sgR], writes=[sgR])
                                if X == 1:
                                    P.emit(pool, lambda e, sg=sg, oc=oc: e.tensor_tensor(out=mm_[:, oc, :], in0=mm_[:, oc, :], in1=sg[:, 0:T], op=ALU.add),
                                           reads=[mR[oc], sgR], writes=[mR[oc]])
                                else:
                                    P.emit(pool, lambda e, sg=sg, oc=oc: e.tensor_tensor(out=mbf[:, oc, :], in0=mm_[:, oc, :], in1=sg[:, 0:T], op=ALU.add),
                                           reads=[mR[oc], sgR], writes=[mbR[oc]])
                            if X < 2:
                                P.pump("dve", 2 * PUMP_P)

                ada_hook("after_G")
                if l == NL - 1 and it + 1 < NT:
                    P.emit(sp, lambda e: e.dma_start(out=XB[1 - cur][:, :, :], in_=x_d[:, :, t0 + T:t0 + 2 * T]), writes=XS_R[1 - cur], dsem=xS2[1 - cur])

                for half in range(2):
                    wo, woR = next_piece("w", l, P_O + half)
                    obk = [nb() for _ in range(4)]
                    (groups_ilv if half == 0 else groups_seq)(
                        [(obk[j], ps[:, obk[j], :], [(wo[:, kc, j * 128:(j + 1) * 128], mbf[:, kc, :], [woR, mbR[kc]]) for kc in range(KC)]) for j in range(4)])
                    for j in range(4):
                        oc = half * 4 + j
                        b = obk[j]
                        P.emit(dve, lambda e, b=b, oc=oc: e.scalar_tensor_tensor(out=xs[:, oc, :], in0=ps[:, b, :], scalar=g1v[:, oc:oc + 1], in1=xs[:, oc, :],
                                                                                 op0=ALU.mult, op1=ALU.add),
                               reads=[bkR[b], modR[l][4], modR[l][5], xsR[oc]], writes=[xsR[oc]])

                rmsnorm_to(l, At[:, l, 8:16], sh2, hb, hR, xreads=[AR2[l], modR[l][6], modR[l][7]])
                for i in range(11):
                    wt, wr_ = next_piece("w", l, P_FI + i)
                    specs, fbk = [], []
                    for jj in range(2):
                        bg_, bu_ = nb(), nb()
                        fbk.append((bg_, bu_))
                        specs.append((bg_, ps[:, bg_, :], [(wt[:, kc, (2 * jj) * 128:(2 * jj + 1) * 128], hb[:, kc, :], [wr_, hR[kc]]) for kc in range(KC)]))
                        specs.append((bu_, ps[:, bu_, :], [(wt[:, kc, (2 * jj + 1) * 128:(2 * jj + 2) * 128], hb[:, kc, :], [wr_, hR[kc]]) for kc in range(KC)]))
                    (groups_ilv if i == 0 else groups_seq)(specs)
                    for jj in range(2):
                        j = 2 * i + jj
                        bg_, bu_ = fbk[jj]
                        sg, sgR = newtmp()
                        P.emit(act, lambda e, sg=sg, bg_=bg_: e.activation(out=sg[:, 0:T], in_=ps[:, bg_, :], func=AF.Silu), reads=[bkR[bg_]], writes=[sgR])
                        P.emit(dve, lambda e, sg=sg, bu_=bu_, j=j: e.tensor_tensor(out=ff[:, j, :], in0=ps[:, bu_, :], in1=sg[:, 0:T], op=ALU.mult),
                               reads=[bkR[bu_], sgR], writes=[ffR[j]])
                    ada_hook("ffn_in_%d" % i)
                if l == NL - 1 and it + 1 < NT:
                    rmsnorm_to(0, At[:, 0, 0:8], modt[:, 0, 0:8], hb, hR, xreads=[AR1[0], modR[0][0], modR[0][1]],
                               xs_=XB[1 - cur], xsR_=XS_R[1 - cur])
                for ch in range(2):
                    banks = [nb() for _ in range(4)]
                    for kp in range(3):
                        wt, wr_ = next_piece("w", l, P_FO + ch * 3 + kp)
                        nk = 8 if kp < 2 else 6
                        for j in range(4):
                            for kcl in range(nk):
                                kc = kp * 8 + kcl
                                b = banks[j]
                                P.emit(pe, lambda e, wt=wt, kcl=kcl, kc=kc, j=j, b=b: e.matmul(ps[:, b, :], wt[:, kcl, j * 128:(j + 1) * 128], ff[:, kc, :],
                                                                                               start=(kc == 0), stop=(kc == FKC - 1)),
                                       reads=[wr_, ffR[kc]], writes=[bkR[b]] if kc == 0 else [], inc=(kcl == nk - 1))
                        if ch == 0 and kp == 0:
                            ada_hook("ffn_out_0")
                    for j in range(4):
                        oc = ch * 4 + j
                        b = banks[j]
                        bkR[b].w = (pe.key, pe.cnt)
                        P.emit(dve, lambda e, b=b, oc=oc: e.scalar_tensor_tensor(out=xs[:, oc, :], in0=ps[:, b, :], scalar=g2v[:, oc:oc + 1], in1=xs[:, oc, :],
                                                                                 op0=ALU.mult, op1=ALU.add),
                               reads=[bkR[b], modR[l][10], modR[l][11], xsR[oc]], writes=[xsR[oc]])

            def final_norm(xb=XB[cur], xbR=XS_R[cur], t0=t0):
                rmsnorm_to(0, fgt, None, ost2, osR2, xs_=xb, xsR_=xbR, ri=1)
                P.emit(sp, lambda e: e.dma_start(out=y_d[:, :, t0:t0 + T], in_=ost2), reads=osR2, dsem=oS)
            pending_final.append(final_norm)
        while pending_final:
            pending_final.pop(0)()
        sp.be.wait_ge(oS.sem, oS.cnt)
    return nc


_OFF = dict(b_a=0, c_a=1024, x_a=2048, u=3072, v=4096, a_c=5120, g_c=6144, gate_a=7168, gate_b=8192, gate_c=9216)


def _pack(Wm, cols):
    sub = Wm[:, cols]
    return np.ascontiguousarray(sub.reshape(KC, 128, 512).transpose(1, 0, 2)).reshape(128, 4096)


def _sl(name, oc):
    return np.arange(_OFF[name] + oc * 128, _OFF[name] + (oc + 1) * 128)


def _pack_layer(w_in, w_a, w_b, w_c, w_o, w_fi, w_fo, conv_w):
    slots = []
    for oc in range(8):
        slots += [_sl("a_c", oc), _sl("g_c", oc)]
    for oc in range(8):
        slots.append(_sl("u", oc))
    for oc in range(8):
        slots.append(_sl("v", oc))
    for oc in range(8):
        slots += [_sl("c_a", oc), _sl("x_a", oc), _sl("b_a", oc)]
    pieces = [_pack(w_in, np.concatenate(slots[4 * i:4 * i + 4])) for i in range(14)]
    for X, Wout in (("gate_a", w_a), ("gate_b", w_b), ("gate_c", w_c)):
        for half in range(2):
            pieces.append(_pack(w_in, np.concatenate([_sl(X, half * 4 + j) for j in range(4)])))
            pieces.append(_pack(Wout, np.arange(half * 512, half * 512 + 512)))
    for half in range(2):
        pieces.append(_pack(w_o, np.arange(half * 512, half * 512 + 512)))
    for i in range(11):
        cols = []
        for j in (2 * i, 2 * i + 1):
            cols += [np.arange(j * 128, (j + 1) * 128), DFF + np.arange(j * 128, (j + 1) * 128)]
        pieces.append(_pack(w_fi, np.concatenate(cols)))
    for ch in range(2):
        for kp in range(3):
            nk = 8 if kp < 2 else 6
            blk = np.zeros((128, KC, 512), np.float32)
            rows = w_fo[kp * 1024:kp * 1024 + nk * 128, ch * 512:(ch + 1) * 512]
            blk[:, :nk, :] = rows.reshape(nk, 128, 512).transpose(1, 0, 2)
            pieces.append(blk.reshape(128, 4096))
    for i in range(NCP):
        oc = NDV + i
        blk = np.zeros((128, 32, 128), np.float32)
        idx = np.arange(128)
        for k in range(31):
            blk[idx, k, idx] = conv_w[k, oc * 128:(oc + 1) * 128]
        pieces.append(blk.reshape(128, 4096))
    assert len(pieces) == NPC
    return np.stack(pieces)


def _pcol(v):
    v = np.asarray(v)
    lead = v.shape[:-1]
    n = v.shape[-1] // 128
    r = v.reshape(lead + (n, 128))
    r = np.moveaxis(r, -1, 0)
    return r.reshape(128, -1)


_CACHE = {}
_LOG = None


def _host_prep(inputs, NL):
    f32 = np.float32
    wpk = np.stack([_pack_layer(inputs["w_in"][l], inputs["w_a_out"][l], inputs["w_b_out"][l], inputs["w_c_out"][l],
                                inputs["w_o"][l], inputs["w_ffn_in"][l], inputs["w_ffn_out"][l], inputs["cfm_conv_w"][l]) for l in range(NL)]).astype(f32)
    wada = np.stack([np.stack([_pack(inputs["w_ada"][l], np.arange(j * 512, (j + 1) * 512)) for j in range(NADA)]) for l in range(NL)]).astype(f32)
    cv = np.zeros((NL, 128, NCV), f32)
    for l in range(NL):
        cv[l, :, CV_N1G:CV_N1G + 8] = _pcol(inputs["norm1_g"][l])
        cv[l, :, CV_N2G:CV_N2G + 8] = _pcol(inputs["norm2_g"][l])
        cv[l, :, CV_WSH:CV_WSH + 24] = _pcol(inputs["w_short"][l])
        cv[l, :, CV_CW:CV_CW + 248] = _pcol(inputs["cfm_conv_w"][l])
        cv[l, :, CV_CB:CV_CB + 8] = _pcol(inputs["cfm_conv_b"][l])
        cv[l, :, CV_CLG:CV_CLG + 8] = _pcol(inputs["cfm_ln_g"][l])
        cv[l, :, CV_CLB:CV_CLB + 8] = _pcol(inputs["cfm_ln_b"][l])
        cv[l, :, CV_BADA:CV_BADA + 48] = _pcol(inputs["b_ada"][l])
    fg = _pcol(inputs["final_g"]).astype(f32)
    sln = np.stack([np.concatenate([np.broadcast_to(inputs["sgu_ln_g"][l][None, :], (128, 1024)),
                                    np.broadcast_to(inputs["sgu_ln_b"][l][None, :], (128, 1024))], axis=1) for l in range(NL)]).astype(f32)
    wsg = np.ascontiguousarray(np.transpose(inputs["w_sgu"][:NL], (0, 3, 1, 2))).reshape(NL, 128, 1024).astype(f32)
    s_i = np.arange(128)[:, None]
    t_i = np.arange(128)[None, :]
    msk = np.tile((s_i <= t_i).astype(f32), (1, 8))
    bsg = np.ascontiguousarray(inputs["b_sgu"][:NL]).reshape(NL, 1, 1024).astype(f32)
    return dict(wpk=wpk, wada=wada, cv=cv, fg=fg, sln=np.ascontiguousarray(sln), wsg=wsg, msk=msk, bsg=bsg)


def kernel(**inputs):
    inputs = {k: np.asarray(v) for k, v in inputs.items()}
    x = inputs["x"]
    B, S, _ = x.shape
    NL = inputs["w_in"].shape[0]
    key = (S, NL)
    if key not in _CACHE:
        _CACHE[key] = build_program(S, NL)
    nc = _CACHE[key]
    shared = _host_prep(inputs, NL)
    in_maps = []
    for b in range(B):
        m = dict(shared)
        m["x"] = np.ascontiguousarray(x[b].T.reshape(KC, 128, S).transpose(1, 0, 2)).astype(np.float32)
        m["cc"] = _pcol(inputs["c"][b]).astype(np.float32)
        in_maps.append(m)
    res = run_bass_kernel_spmd(nc, in_maps, core_ids=list(range(B)))
    out = np.empty((B, S, D), np.float32)
    for b in range(B):
        y = np.asarray(res.results[b]["y"])
        out[b] = y.transpose(2, 1, 0).reshape(S, D)
    return out
```

```python
import numpy as np
from contextlib import ExitStack
import concourse.bass as bass
import concourse.mybir as mybir
from concourse.bass_utils import run_bass_kernel_spmd

F32 = mybir.dt.float32
BF16 = mybir.dt.bfloat16
AF = mybir.ActivationFunctionType
ALU = mybir.AluOpType

D = 1024
KC = 8
DFF = 2816
FKC = 22
EPS = 1e-6
NCP = 4
NDV = 8 - NCP
NPB = 45
NPC = NPB + NCP
P_C, P_U, P_V, P_A, P_G, P_O, P_FI, P_FO = 0, 4, 6, 8, 14, 26, 28, 39
NADA = 12
CV_N1G, CV_N2G, CV_WSH, CV_CW, CV_CB, CV_CLG, CV_CLB, CV_BADA = 0, 8, 16, 40, 288, 296, 304, 312
NCV = 360
GH = 32


class _Eng:
    def __init__(self, name, be, sem, key, is_pe=False):
        self.name, self.be, self.sem, self.key, self.is_pe = name, be, sem, key, is_pe
        self.cnt = 0
        self.known = {}


class _DSem:
    def __init__(self, sem, key):
        self.sem, self.key, self.cnt = sem, key, 0


class _Res:
    __slots__ = ("name", "region", "lo", "hi", "w", "r")

    def __init__(self, name, region=None, lo=0, hi=0):
        self.name, self.region, self.lo, self.hi = name, region, lo, hi
        self.w = None
        self.r = {}
        if region is not None:
            region.append(self)

    def ov(self):
        if self.region is None:
            return (self,)
        return [o for o in self.region if o.lo < self.hi and self.lo < o.hi]


class _Prog:
    def __init__(self, nc, es):
        self.nc = nc
        self.es = es
        self.sems = {}
        self.nsem = 0
        self.pe = self._eng("pe", nc.tensor, True)
        self.act = self._eng("act", nc.scalar)
        self.dve = self._eng("dve", nc.vector)
        self.pool = self._eng("pool", nc.gpsimd)
        self.sp = _Eng("sp", nc.sync, None, -1)
        self.bgq = {"dve": [], "pool": [], "vn": []}
        self.log = None

    def _newsem(self, name):
        s = self.es.enter_context(self.nc.semaphore(name))
        k = self.nsem
        self.nsem += 1
        self.sems[k] = s
        return s, k

    def _eng(self, name, be, is_pe=False):
        s, k = self._newsem("e_" + name)
        return _Eng(name, be, s, k, is_pe)

    def dsem(self, name):
        s, k = self._newsem(name)
        return _DSem(s, k)

    def emit(self, eng, fn, reads=(), writes=(), inc=True, dsem=None):
        need = {}
        for r in reads:
            for o in r.ov():
                if o.w is not None and need.get(o.w[0], 0) < o.w[1]:
                    need[o.w[0]] = o.w[1]
        for w in writes:
            for o in w.ov():
                if o.w is not None and need.get(o.w[0], 0) < o.w[1]:
                    need[o.w[0]] = o.w[1]
                for k, v in o.r.items():
                    if need.get(k, 0) < v:
                        need[k] = v
        waits = []
        for k, v in need.items():
            if eng.is_pe and k == eng.key:
                continue
            if eng.known.get(k, 0) >= v:
                continue
            eng.known[k] = v
            waits.append((k, v))
        attach = None
        if waits and not eng.is_pe:
            attach = waits[-1]
        for k, v in waits:
            if (k, v) is not attach:
                eng.be.wait_ge(self.sems[k], v)
        ins = fn(eng.be)
        if attach is not None:
            ins._wait_ge(self.sems[attach[0]], attach[1])
        if dsem is not None:
            dsem.cnt += 16
            ins.then_inc(dsem.sem, 16)
            ev = (dsem.key, dsem.cnt)
        elif inc:
            eng.cnt += 1
            ins.then_inc(eng.sem, 1)
            ev = (eng.key, eng.cnt)
        else:
            ev = (eng.key, eng.cnt + 1)
        if self.log is not None:
            self.log.append((eng.name, waits, (ev[0], 16 if dsem is not None else (1 if inc else 0)), dsem is not None))
        for r in reads:
            if r.r.get(ev[0], 0) < ev[1]:
                r.r[ev[0]] = ev[1]
        for w in writes:
            w.w = ev
            w.r = {}
        return ev

    def pump(self, name, n):
        q = self.bgq[name]
        for _ in range(min(n, len(q))):
            q.pop(0)()

    def flush(self):
        for name in ("dve", "pool"):
            self.pump(name, len(self.bgq[name]))


def build_program(S, NL, T=512, NBUF=5, NTMP=10, PUMP=3, PUMP_A=1, PUMP_P=1, WARM=16):
    NT = S // T
    TC = T // 128
    nc = bass.Bass("TRN2", target_bir_lowering=False)
    dt = nc.dram_tensor
    x_d = dt("x", [128, KC, S], F32, kind="ExternalInput").ap()
    y_d = dt("y", [128, KC, S], F32, kind="ExternalOutput").ap()
    wpk_d = dt("wpk", [NL, NPC, 128, 4096], F32, kind="ExternalInput").ap()
    wada_d = dt("wada", [NL, NADA, 128, 4096], F32, kind="ExternalInput").ap()
    cv_d = dt("cv", [NL, 128, NCV], F32, kind="ExternalInput").ap()
    fg_d = dt("fg", [128, KC], F32, kind="ExternalInput").ap()
    cc_d = dt("cc", [128, KC], F32, kind="ExternalInput").ap()
    sln_d = dt("sln", [NL, 128, 2048], F32, kind="ExternalInput").ap()
    wsg_d = dt("wsg", [NL, 128, 1024], F32, kind="ExternalInput").ap()
    msk_d = dt("msk", [128, 1024], F32, kind="ExternalInput").ap()
    bsg_d = dt("bsg", [NL, 1, 1024], F32, kind="ExternalInput").ap()
    scr_d = dt("scr", [NL, NPC, 128, 4096], BF16, kind="Internal").ap()

    with ExitStack() as es:
        P = _Prog(nc, es)
        if _LOG is not None:
            P.log = _LOG
        pe, act, dve, pool, sp = P.pe, P.act, P.dve, P.pool, P.sp

        def sb(name, shape, dtype):
            return es.enter_context(nc.sbuf_tensor(name, shape, dtype))

        XB = [sb("xb%d" % i, [128, KC, T], F32) for i in range(2)]
        XREG = [[], []]
        XS_R = [[_Res("xs%d_%d" % (i, k), XREG[i], k * T * 4, (k + 1) * T * 4) for k in range(KC)] for i in range(2)]
        CA_R = [[_Res("ca%d_%d" % (i, k), XREG[i], k * T * 4, (k + 1) * T * 4) for k in range(KC)] for i in range(2)]
        YC_R = [[_Res("yc%d_%d" % (i, k), XREG[i], k * T * 4, k * T * 4 + T * 2) for k in range(KC)] for i in range(2)]
        xS2 = [P.dsem("xS%d" % i) for i in range(2)]
        hb = sb("hb", [128, KC, T], BF16)
        hR = [_Res("h%d" % k) for k in range(KC)]
        wring = [sb("wr%d" % i, [128, KC, 512], BF16) for i in range(NBUF)]
        wR = [_Res("wr%d" % i) for i in range(NBUF)]
        wld = [P.dsem("wld%d" % i) for i in range(NBUF)]
        wlds = [P.dsem("wlds%d" % i) for i in range(NBUF)]
        wst = [P.dsem("wst%d" % i) for i in range(NBUF)]
        gluf = sb("gluf", [128, max(NDV, 1), GH + T], F32)
        glub = sb("glub", [128, max(NCP, 1), GH + T], BF16)
        glR = [_Res("gl%d" % k) for k in range(KC)]
        G1 = []
        g1 = sb("g1b", [128, 12 * T], F32)
        gu = g1[:, 0:8 * T].rearrange("p (k t) -> p k t", k=KC)
        ya = g1[:, 8 * T:12 * T].bitcast(BF16).rearrange("p (k t) -> p k t", k=KC)
        ff = g1[:, 0:11 * T].bitcast(BF16).rearrange("p (k t) -> p k t", k=FKC)
        ost = g1[:, 0:8 * T].rearrange("p (k t) -> p k t", k=KC)
        guR = [_Res("gu%d" % k, G1, k * T * 4, (k + 1) * T * 4) for k in range(KC)]
        yaR = [_Res("ya%d" % k, G1, 8 * T * 4 + k * T * 2, 8 * T * 4 + (k + 1) * T * 2) for k in range(KC)]
        ffR = [_Res("ff%d" % k, G1, k * T * 2, (k + 1) * T * 2) for k in range(FKC)]
        osR = [_Res("os%d" % k, G1, k * T * 4, (k + 1) * T * 4) for k in range(KC)]
        g2 = sb("g2", [128, 8 * T], F32)
        G2 = []
        gv = g2[:, :].rearrange("p (c f) -> p c f", c=TC)
        mm_ = g2[:, :].rearrange("p (k t) -> p k t", k=KC)
        gvR = [_Res("gv%d" % c, G2, c * 4096, (c + 1) * 4096) for c in range(TC)]
        mR = [_Res("m%d" % k, G2, k * T * 4, (k + 1) * T * 4) for k in range(KC)]
        ost2 = g2[:, :].rearrange("p (k t) -> p k t", k=KC)
        osR2 = [_Res("os2_%d" % k, G2, k * T * 4, (k + 1) * T * 4) for k in range(KC)]
        pending_final = []
        g3 = sb("g3", [128, 8 * T], BF16)
        G3 = []
        nrm = g3[:, :].rearrange("p (c f) -> p c f", c=TC)
        mbf = g3[:, :].rearrange("p (k t) -> p k t", k=KC)
        nrR = [_Res("nr%d" % c, G3, c * 2048, (c + 1) * 2048) for c in range(TC)]
        mbR = [_Res("mb%d" % k, G3, k * T * 2, (k + 1) * T * 2) for k in range(KC)]
        yb = sb("yb", [128, KC, T], BF16)
        ybR = [_Res("yb%d" % k) for k in range(KC)]
        tmps = [sb("tmp%d" % i, [128, T + 8], F32) for i in range(NTMP)]
        tmpR = [_Res("tmp%d" % i) for i in range(NTMP)]
        tmpi = [0]

        def newtmp():
            i = tmpi[0] % NTMP
            tmpi[0] += 1
            return tmps[i], tmpR[i]

        rs = [sb("rs%d" % i, [128, T], F32) for i in range(2)]
        rsR = [_Res("rs%d" % i) for i in range(2)]
        mean_t = sb("mean_t", [128, T], F32)
        meanR = _Res("mean_t")
        cvt = sb("cvt", [128, NL, NCV], F32)
        cvR = _Res("cvt")
        fgt = sb("fgt", [128, KC], F32)
        cct = sb("cct", [128, KC], F32)
        cab = sb("cab", [128, KC], BF16)
        cabR = _Res("cab")
        modt = sb("modt", [128, NL, 48], F32)
        modR = [[_Res("mod%d_%d" % (l, j)) for j in range(NADA)] for l in range(NL)]
        At = sb("At", [128, NL, 16], F32)
        AR1 = [_Res("A1_%d" % l) for l in range(NL)]
        AR2 = [_Res("A2_%d" % l) for l in range(NL)]
        ones_f = sb("ones_f", [128, 128], F32)
        ones_b = sb("ones_b", [128, 128], BF16)
        epst = sb("epst", [128, 1], F32)
        dmy = sb("dmy", [128, 2], F32)
        dmyR = _Res("dmy")
        ones_m = sb("ones_m", [128, 128], BF16)
        onesR = _Res("ones")
        wsgt = g2[:, 0:1024]
        mskt = g2[:, 1024:2048]
        wsR = _Res("wsgt", G2, 0, 4096)
        mskR = _Res("mskt", G2, 4096, 8192)
        wmt = sb("wmt", [128, NL, 1024], BF16)
        wmR = _Res("wmt")
        bsf = g2[0:1, 2048:2048 + NL * 1024].rearrange("o (l n) -> o l n", l=NL)
        bsfR = _Res("bsf", G2, 8192, 8192 + NL * 4096)
        bsh = sb("bsh", [1, NL, 1024], BF16)
        bsl = sb("bsl", [1, NL, 1024], BF16)
        bsR = _Res("bs")
        slnt = sb("slnt", [128, 2048], F32)
        slnR = _Res("sln")
        slnS = P.dsem("slnS")
        gst = sb("gst", [128, NL, max(NDV, 1), GH], F32)
        gstb = sb("gstb", [128, NL, max(NCP, 1), GH], BF16)
        gstR = [_Res("gst%d" % l) for l in range(NL)]
        pst = sb("pst", [128, NL, KC, 2], F32)
        pstR = [_Res("pst%d" % l) for l in range(NL)]
        bst = sb("bst", [128, TC, 2, 6], F32)
        bag = sb("bag", [128, TC, 2], F32)
        brs = sb("brs", [128, TC], F32)
        bnR = [_Res("bn%d" % c) for c in range(TC)]
        brsR = _Res("brs")
        constR = _Res("const")
        oS = P.dsem("oS")
        cS = P.dsem("cS")
        scrR = [[_Res("scr%d_%d" % (l, q)) for q in range(NPC)] for l in range(NL)]

        ps = es.enter_context(nc.psum_tensor("ps", [128, 8, 512], F32))
        bkR = [_Res("bank%d" % i) for i in range(8)]
        bki = [0]

        def nb():
            i = bki[0] % 8
            bki[0] += 1
            return i

        cdmas = [
            (cvt[:, :, :], cv_d.rearrange("l p n -> p l n")),
            (fgt[:, :], fg_d), (cct[:, :], cc_d), (mskt, msk_d),
            (bsf, bsg_d.rearrange("l o n -> o l n")),
        ]
        for o, i in cdmas:
            P.emit(sp, lambda e, o=o, i=i: e.dma_start(out=o, in_=i), dsem=cS)
        constR.w = (cS.key, cS.cnt)
        cvR.w = constR.w
        mskR.w = constR.w
        bsfR.w = constR.w
        P.emit(pool, lambda e: e.memset(ones_f[:, :], 1.0 / D), writes=[onesR])
        P.emit(pool, lambda e: e.memset(ones_b[:, :], 1.0), writes=[onesR])
        P.emit(pool, lambda e: e.memset(epst[:, :], EPS), writes=[onesR])
        P.emit(pool, lambda e: e.memset(ones_m[:, :], 1.0 / D), writes=[onesR])
        for l in range(NL):
            P.emit(pool, lambda e, l=l: e.memset(gst[:, l, :, :], 0.0), writes=[gstR[l]])
            P.emit(pool, lambda e, l=l: e.memset(gstb[:, l, :, :], 0.0), writes=[gstR[l]])
            P.emit(pool, lambda e, l=l: e.memset(pst[:, l, :, :], 0.0), writes=[pstR[l]])
        P.emit(act, lambda e: e.activation(out=cab[:, :], in_=cct[:, :], func=AF.Silu), reads=[constR], writes=[cabR])
        wsS = P.dsem("wsS")
        for l in range(NL):
            P.emit(sp, lambda e, l=l: e.dma_start(out=wsgt, in_=wsg_d[l]), writes=[wsR], dsem=wsS)
            P.emit(dve, lambda e, l=l: e.tensor_tensor(out=wmt[:, l, :], in0=wsgt, in1=mskt, op=ALU.mult),
                   reads=[wsR, mskR], writes=[wmR])
        P.emit(act, lambda e: e.activation(out=bsh[:, :, :], in_=bsf, func=AF.Copy), reads=[bsfR], writes=[bsR])
        P.emit(dve, lambda e: e.tensor_tensor(out=bsl[:, :, :], in0=bsf, in1=bsh[:, :, :], op=ALU.subtract),
               reads=[bsfR, bsR], writes=[bsR])

        def ada_sched(l, tag):
            if tag == "start" and l == 0:
                return [(0, j) for j in range(4)]
            if tag == "after_C" and l == 0:
                return [(0, 4), (0, 5)]
            if tag == "after_V" and l == 0:
                return [(0, j) for j in range(6, 10)]
            if tag == "after_G" and l == 0:
                return [(0, 10), (0, 11)]
            if l + 1 < NL:
                if tag.startswith("ffn_in_"):
                    return [(l + 1, int(tag[7:]))]
                if tag == "ffn_out_0":
                    return [(l + 1, 11)]
            return []

        def layer_seq(it, l):
            seq = []

            def ada(tag):
                if it == 0:
                    seq.extend(("ada", ll, j, it) for (ll, j) in ada_sched(l, tag))

            def w(qs):
                seq.extend(("w", l, q, it) for q in qs)
            ada("start")
            w(range(P_C, P_U))
            ada("after_C")
            w(range(P_U, P_V))
            w(range(P_V, P_A))
            w(range(NPB, NPC))
            ada("after_V")
            w(range(P_A, P_G))
            w(range(P_G, P_O))
            ada("after_G")
            w(range(P_O, P_FI))
            for i in range(11):
                w([P_FI + i])
                ada("ffn_in_%d" % i)
            for i in range(6):
                w([P_FO + i])
                if i == 0:
                    ada("ffn_out_0")
            return seq

        stream = []
        for it in range(NT):
            for l in range(NL):
                stream.extend(layer_seq(it, l))
        nload = [0]

        def issue_load():
            i = nload[0]
            if i >= len(stream):
                return
            nload[0] += 1
            kind, l, q, it = stream[i]
            s = i % NBUF
            dst = wring[s][:, :, :].rearrange("p k c -> p (k c)")
            if kind == "ada":
                P.emit(pool, lambda e: e.dma_start(out=dst, in_=wada_d[l, q]), writes=[wR[s]], dsem=wlds[s])
            elif it == 0:
                P.emit(pool, lambda e: e.dma_start(out=dst, in_=wpk_d[l, q]), writes=[wR[s]], dsem=wlds[s])
                P.emit(sp, lambda e: e.dma_start(out=scr_d[l, q], in_=dst), reads=[wR[s]], writes=[scrR[l][q]], dsem=wst[s])
            else:
                P.emit(sp, lambda e: e.dma_start(out=dst, in_=scr_d[l, q]), reads=[scrR[l][q]], writes=[wR[s]], dsem=wld[s])

        nuse = [0]

        def next_piece(kind, l, q):
            i = nuse[0]
            nuse[0] += 1
            assert stream[i][0] == kind and stream[i][1] == l and stream[i][2] == q, (stream[i], kind, l, q)
            issue_load()
            s = i % NBUF
            return wring[s], wR[s]

        for _ in range(NBUF - 2):
            issue_load()

        def group(bank, out_ap, items, inc_all=False):
            n = len(items)
            for i, (lt, rh, rd) in enumerate(items):
                last = i == n - 1
                P.emit(pe, lambda e, lt=lt, rh=rh, i=i, last=last: e.matmul(out_ap, lt, rh, start=(i == 0), stop=last),
                       reads=rd, writes=[bkR[bank]] if i == 0 else [], inc=(last or inc_all))

        def groups_ilv(specs):
            n = len(specs[0][2])
            for i in range(n):
                for (bank, out_ap, items) in specs:
                    lt, rh, rd = items[i]
                    last = i == n - 1
                    P.emit(pe, lambda e, lt=lt, rh=rh, out_ap=out_ap, last=last: e.matmul(out_ap, lt, rh, start=(i == 0), stop=last),
                           reads=rd, writes=[bkR[bank]] if i == 0 else [], inc=last)
                    if last:
                        bkR[bank].w = (pe.key, pe.cnt)

        def groups_seq(specs):
            for (bank, out_ap, items) in specs:
                group(bank, out_ap, items)
                group_done(bank)

        def group_done(bank):
            bkR[bank].w = (pe.key, pe.cnt)

        def after_group():
            P.pump("dve", PUMP)

        def warm(n):
            if n <= 0:
                return
            b = nb()
            for i in range(n):
                P.emit(pe, lambda e: e.matmul(ps[:, b, :], ones_m[:, :], wmt[:, 0, 0:512], start=True, stop=True),
                       reads=[onesR, wmR], writes=[bkR[b]] if i == 0 else [], inc=False)

        def rmsnorm_to(l, A_ap, sh_ap, dst, dstR, xreads=(), xs_=None, xsR_=None, ri=0):
            xs_ = xs if xs_ is None else xs_
            xsR_ = xsR if xsR_ is None else xsR_
            b = nb()
            for kc in range(KC):
                t, tR = newtmp()
                tb = t[:, 0:T // 2].bitcast(BF16)
                P.emit(act, lambda e, tb=tb, kc=kc: e.activation(out=tb, in_=xs_[:, kc, :], func=AF.Square),
                       reads=[xsR_[kc]], writes=[tR])
                P.emit(pe, lambda e, tb=tb, kc=kc: e.matmul(ps[:, b, :], ones_m[:, :], tb, start=(kc == 0), stop=(kc == KC - 1)),
                       reads=[tR, onesR], writes=[bkR[b]] if kc == 0 else [], inc=True)
            group_done(b)
            if sh_ap is not None:
                warm(WARM)
            P.emit(act, lambda e: e.activation(out=dmy[:, 0:1], in_=epst[:, 0:1], func=AF.Ln), reads=[onesR], writes=[dmyR])
            r, rR = rs[ri], rsR[ri]
            P.emit(act, lambda e: e.activation(out=r[:, :], in_=ps[:, b, :], func=AF.Ln, bias=epst[:, 0:1], scale=1.0), reads=[bkR[b], onesR], writes=[rR])
            P.emit(act, lambda e: e.activation(out=r[:, :], in_=r[:, :], func=AF.Exp, scale=-0.5), reads=[rR], writes=[rR])
            for kc in range(KC):
                t, tR = newtmp()
                eng = dve
                P.emit(eng, lambda e, t=t, kc=kc: e.tensor_tensor(out=t[:, 0:T], in0=xs_[:, kc, :], in1=r[:, :], op=ALU.mult),
                       reads=[xsR_[kc], rR], writes=[tR])
                if sh_ap is not None:
                    P.emit(act, lambda e, t=t, kc=kc: e.activation(out=dst[:, kc, :], in_=t[:, 0:T], func=AF.Identity,
                                                                   scale=A_ap[:, kc:kc + 1], bias=sh_ap[:, kc:kc + 1]),
                           reads=[tR] + list(xreads), writes=[dstR[kc]])
                else:
                    P.emit(act, lambda e, t=t, kc=kc: e.activation(out=dst[:, kc, :], in_=t[:, 0:T], func=AF.Identity,
                                                                   scale=A_ap[:, kc:kc + 1], bias=0.0),
                           reads=[tR, constR], writes=[dstR[kc]])

        def cvcol(l, base, k):
            return cvt[:, l, base + k:base + k + 1]

        P.emit(sp, lambda e: e.dma_start(out=XB[0][:, :, :], in_=x_d[:, :, 0:T]), writes=XS_R[0], dsem=xS2[0])
        for it in range(NT):
            t0 = it * T
            cur = it % 2
            xs, xsR = XB[cur], XS_R[cur]
            cacc, caR = XB[1 - cur], CA_R[1 - cur]
            yc = cacc[:, :, 0:T // 2].bitcast(BF16)
            ycR = YC_R[1 - cur]
            for l in range(NL):
                P.emit(sp, lambda e: e.dma_start(out=slnt[:, :], in_=sln_d[l]), writes=[slnR], dsem=slnS)
                def ada_hook(tag, l=l, it=it):
                    if it != 0:
                        return
                    for (lt, j) in ada_sched(l, tag):
                        wt, wr_ = next_piece("ada", lt, j)
                        b = nb()
                        for c4 in range(4):
                            for kc in range(KC):
                                P.emit(pe, lambda e, wt=wt, c4=c4, kc=kc: e.matmul(
                                    ps[:, b, c4:c4 + 1], wt[:, kc, c4 * 128:(c4 + 1) * 128], cab[:, kc:kc + 1],
                                    start=(kc == 0), stop=(kc == KC - 1)),
                                    reads=[wr_, cabR], writes=[bkR[b]] if (c4 == 0 and kc == 0) else [], inc=(kc == KC - 1))
                        group_done(b)
                        P.emit(dve, lambda e, lt=lt, j=j, b=b: e.tensor_tensor(out=modt[:, lt, 4 * j:4 * j + 4], in0=ps[:, b, 0:4],
                                                                              in1=cvt[:, lt, CV_BADA + 4 * j:CV_BADA + 4 * j + 4], op=ALU.add),
                               reads=[bkR[b], cvR], writes=[modR[lt][j]])
                        if j == 3:
                            P.emit(dve, lambda e, lt=lt: e.scalar_tensor_tensor(out=At[:, lt, 0:8], in0=modt[:, lt, 8:16], scalar=1.0, in1=cvt[:, lt, CV_N1G:CV_N1G + 8],
                                                                               op0=ALU.add, op1=ALU.mult), reads=[modR[lt][2], modR[lt][3], cvR], writes=[AR1[lt]])
                        if j == 9:
                            P.emit(dve, lambda e, lt=lt: e.scalar_tensor_tensor(out=At[:, lt, 8:16], in0=modt[:, lt, 32:40], scalar=1.0, in1=cvt[:, lt, CV_N2G:CV_N2G + 8],
                                                                               op0=ALU.add, op1=ALU.mult), reads=[modR[lt][8], modR[lt][9], cvR], writes=[AR2[lt]])

                ada_hook("start")
                sh1 = modt[:, l, 0:8]
                g1v = modt[:, l, 16:24]
                sh2 = modt[:, l, 24:32]
                g2v = modt[:, l, 40:48]

                if not (it > 0 and l == 0):
                    rmsnorm_to(l, At[:, l, 0:8], sh1, hb, hR, xreads=[AR1[l], modR[l][0], modR[l][1]])

                if NDV:
                    P.emit(pool, lambda e: e.tensor_copy(out=gluf[:, :, 0:GH], in_=gst[:, l, :, :]), reads=[gstR[l]], writes=glR[:NDV])
                if NCP:
                    P.emit(pool, lambda e: e.tensor_copy(out=glub[:, :, 0:GH], in_=gstb[:, l, :, :]), reads=[gstR[l]], writes=glR[NDV:])

                o0 = GH - 30
                def c_evac(oc, ba, bg, l=l):
                        t, tR = newtmp()
                        P.emit(act, lambda e, t=t, bg=bg: e.activation(out=t[:, 0:T], in_=ps[:, bg, :], func=AF.Sigmoid), reads=[bkR[bg]], writes=[tR])
                        if oc >= NDV:
                            P.emit(dve, lambda e, t=t, ba=ba, oc=oc: e.tensor_tensor(out=glub[:, oc - NDV, GH:GH + T], in0=ps[:, ba, :], in1=t[:, 0:T], op=ALU.mult),
                                   reads=[bkR[ba], tR], writes=[glR[oc]])
                            return
                        P.emit(dve, lambda e, t=t, ba=ba, oc=oc: e.tensor_tensor(out=gluf[:, oc, GH:GH + T], in0=ps[:, ba, :], in1=t[:, 0:T], op=ALU.mult),
                               reads=[bkR[ba], tR], writes=[glR[oc]])
                        ename = "dve"
                        eng = dve

                        def tap_first(eng=eng, oc=oc, l=l):
                            P.emit(act, lambda e: e.activation(out=cacc[:, oc, :], in_=gluf[:, oc, o0 + 30:o0 + 30 + T], func=AF.Identity,
                                                               scale=cvcol(l, CV_CW, 30 * 8 + oc), bias=cvcol(l, CV_CB, oc)),
                                   reads=[glR[oc], cvR], writes=[caR[oc]])

                        def tap(k, eng=eng, oc=oc, l=l):
                            P.emit(eng, lambda e: e.scalar_tensor_tensor(out=cacc[:, oc, :], in0=gluf[:, oc, o0 + k:o0 + k + T],
                                                                         scalar=cvcol(l, CV_CW, k * 8 + oc), in1=cacc[:, oc, :],
                                                                         op0=ALU.mult, op1=ALU.add),
                                   reads=[glR[oc], cvR, caR[oc]], writes=[caR[oc]])
                        P.bgq[ename].append(tap_first)
                        for k in range(29, -1, -1):
                            P.bgq[ename].append(lambda k=k, tap=tap: tap(k))

                for pr in range(4):
                    wt, wr_ = next_piece("w", l, P_C + pr)
                    specs, cb = [], []
                    for oc in (2 * pr, 2 * pr + 1):
                        s0 = (oc % 2) * 2
                        ba, bg = nb(), nb()
                        cb.append((oc, ba, bg))
                        specs.append((ba, ps[:, ba, :], [(wt[:, kc, s0 * 128:(s0 + 1) * 128], hb[:, kc, :], [wr_, hR[kc]]) for kc in range(KC)]))
                        specs.append((bg, ps[:, bg, :], [(wt[:, kc, (s0 + 1) * 128:(s0 + 2) * 128], hb[:, kc, :], [wr_, hR[kc]]) for kc in range(KC)]))
                    (groups_ilv if pr == 0 else groups_seq)(specs)
                    for (oc, ba, bg) in cb:
                        c_evac(oc, ba, bg)
                ada_hook("after_C")
                while l == 0 and pending_final:
                    pending_final.pop(0)()
                if NDV:
                    P.emit(pool, lambda e: e.tensor_copy(out=gst[:, l, :, :], in_=gluf[:, :, T:T + GH]), reads=glR[:NDV], writes=[gstR[l]])
                if NCP:
                    P.emit(pool, lambda e: e.tensor_copy(out=gstb[:, l, :, :], in_=glub[:, :, T:T + GH]), reads=glR[NDV:], writes=[gstR[l]])

                def pe_conv():
                    for i in range(NCP):
                        oc = NDV + i
                        wt, wr_ = next_piece("w", l, NPB + i)
                        b = nb()
                        group(b, ps[:, b, :], [(wt[:, k // 4, (k % 4) * 128:(k % 4 + 1) * 128], glub[:, i, o0 + k:o0 + k + T], [wr_, glR[oc]]) for k in range(31)])
                        group_done(b)
                        P.emit(act, lambda e, b=b, oc=oc: e.activation(out=cacc[:, oc, :], in_=ps[:, b, :], func=AF.Identity,
                                                                       bias=cvcol(l, CV_CB, oc), scale=1.0), reads=[bkR[b], cvR], writes=[caR[oc]])
                        after_group()
                        P.pump("vn", 2)

                for oc in range(KC):
                    if oc % 4 == 0:
                        wt, wr_ = next_piece("w", l, P_U + oc // 4)
                    s0 = oc % 4
                    b = nb()
                    group(b, ps[:, b, :], [(wt[:, kc, s0 * 128:(s0 + 1) * 128], hb[:, kc, :], [wr_, hR[kc]]) for kc in range(KC)])
                    group_done(b)
                    P.emit(act, lambda e, b=b, oc=oc: e.activation(out=gu[:, oc, :], in_=ps[:, b, :], func=AF.Gelu_apprx_tanh),
                           reads=[bkR[b]], writes=[guR[oc]])
                    after_group()

                wv = [next_piece("w", l, P_V), next_piece("w", l, P_V + 1)]
                for tc in range(TC):
                    for half in range(2):
                        wt, wr_ = wv[half]
                        b = nb()
                        group(b, ps[:, b, :], [(hb[:, kc, tc * 128:(tc + 1) * 128], wt[:, kc, :], [wr_, hR[kc]]) for kc in range(KC)])
                        group_done(b)
                        P.emit(act, lambda e, b=b, tc=tc, half=half: e.activation(out=gv[:, tc, half * 512:(half + 1) * 512], in_=ps[:, b, :],
                                                                                  func=AF.Gelu_apprx_tanh),
                               reads=[bkR[b]], writes=[gvR[tc]])
                        after_group()
                    for half in range(2):
                        P.emit(dve, lambda e, tc=tc, half=half: e.bn_stats(out=bst[:, tc, half, :], in_=gv[:, tc, half * 512:(half + 1) * 512]),
                               reads=[gvR[tc]], writes=[bnR[tc]])
                    P.emit(dve, lambda e, tc=tc: e.bn_aggr(out=bag[:, tc, :], in_=bst[:, tc, :, :]), reads=[bnR[tc]], writes=[bnR[tc]])
                P.emit(act, lambda e: e.activation(out=brs[:, :], in_=bag[:, :, 1], func=AF.Ln, bias=epst[:, 0:1], scale=1.0), reads=bnR + [onesR], writes=[brsR])
                P.emit(act, lambda e: e.activation(out=brs[:, :], in_=brs[:, :], func=AF.Exp, scale=-0.5), reads=[brsR], writes=[brsR])
                for tc in range(TC):
                    def vn1(tc=tc):
                        P.emit(dve, lambda e: e.scalar_tensor_tensor(out=gv[:, tc, :], in0=gv[:, tc, :], scalar=bag[:, tc, 0:1], in1=slnt[:, 0:1024],
                                                                     op0=ALU.subtract, op1=ALU.mult),
                               reads=[gvR[tc], bnR[tc], slnR], writes=[gvR[tc]])

                    def vn2(tc=tc):
                        P.emit(dve, lambda e: e.scalar_tensor_tensor(out=nrm[:, tc, :], in0=gv[:, tc, :], scalar=brs[:, tc:tc + 1], in1=slnt[:, 1024:2048],
                                                                     op0=ALU.mult, op1=ALU.add),
                               reads=[gvR[tc], brsR, slnR], writes=[nrR[tc]])
                    P.bgq["vn"].append(vn1)
                    P.bgq["vn"].append(vn2)

                pe_conv()
                ada_hook("after_V")

                P_aw = [None]
                for oc in range(KC):
                    banks = []
                    for j in range(3):
                        sA = oc * 3 + j
                        if sA % 4 == 0:
                            P_aw[0] = next_piece("w", l, P_A + sA // 4)
                        wt, wr_ = P_aw[0]
                        s0 = sA % 4
                        b = nb()
                        group(b, ps[:, b, :], [(wt[:, kc, s0 * 128:(s0 + 1) * 128], hb[:, kc, :], [wr_, hR[kc]]) for kc in range(KC)])
                        group_done(b)
                        banks.append(b)
                    b0, b1, b2 = banks
                    ca_, caR_ = newtmp()
                    P.emit(act, lambda e, ca_=ca_, b0=b0: e.activation(out=ca_[:, 0:T], in_=ps[:, b0, :], func=AF.Copy), reads=[bkR[b0]], writes=[caR_])
                    pt, ptR = newtmp()
                    P.emit(pool, lambda e, pt=pt, oc=oc: e.tensor_copy(out=pt[:, 0:2], in_=pst[:, l, oc, :]), reads=[pstR[l]], writes=[ptR])
                    P.emit(dve, lambda e, pt=pt, ca_=ca_, b1=b1: e.tensor_tensor(out=pt[:, 2:2 + T], in0=ps[:, b1, :], in1=ca_[:, 0:T], op=ALU.mult),
                           reads=[bkR[b1], caR_], writes=[ptR])
                    P.emit(pool, lambda e, pt=pt, oc=oc: e.tensor_copy(out=pst[:, l, oc, :], in_=pt[:, T:T + 2]), reads=[ptR], writes=[pstR[l]])
                    qt, qtR = newtmp()
                    P.emit(act, lambda e, pt=pt, qt=qt, oc=oc: e.activation(out=qt[:, 0:T], in_=pt[:, 2:2 + T], func=AF.Identity,
                                                                           scale=cvcol(l, CV_WSH, 2 * 8 + oc), bias=0.0), reads=[ptR, cvR], writes=[qtR])
                    for k in (1, 0):
                        P.emit(dve, lambda e, pt=pt, qt=qt, oc=oc, k=k: e.scalar_tensor_tensor(out=qt[:, 0:T], in0=pt[:, k:k + T], scalar=cvcol(l, CV_WSH, k * 8 + oc),
                                                                                              in1=qt[:, 0:T], op0=ALU.mult, op1=ALU.add),
                               reads=[ptR, cvR, qtR], writes=[qtR])
                    P.emit(dve, lambda e, qt=qt, b2=b2, oc=oc: e.tensor_tensor(out=ya[:, oc, :], in0=ps[:, b2, :], in1=qt[:, 0:T], op=ALU.mult),
                           reads=[bkR[b2], qtR], writes=[yaR[oc]])
                    P.pump("dve", PUMP_A)
                    P.pump("vn", 1)

                P.pump("vn", len(P.bgq["vn"]))
                for g in range(KC):
                    b = nb()
                    for tc in range(TC):
                        o_ap = ps[:, b, tc * 128:(tc + 1) * 128]
                        first = tc == 0
                        P.emit(pe, lambda e, o_ap=o_ap, tc=tc, g=g: e.matmul(o_ap, nrm[:, tc, g * 128:(g + 1) * 128], wmt[:, l, g * 128:(g + 1) * 128],
                                                                             start=True, stop=False),
                               reads=[nrR[tc], wmR], writes=[bkR[b]] if first else [], inc=False)
                        P.emit(pe, lambda e, o_ap=o_ap, g=g: e.matmul(o_ap, ones_b[0:1, :], bsh[0:1, l, g * 128:(g + 1) * 128], start=False, stop=False),
                               reads=[onesR, bsR], inc=False)
                        P.emit(pe, lambda e, o_ap=o_ap, g=g: e.matmul(o_ap, ones_b[0:1, :], bsl[0:1, l, g * 128:(g + 1) * 128], start=False, stop=True),
                               reads=[onesR, bsR], inc=(tc == TC - 1))
                    group_done(b)
                    P.emit(dve, lambda e, b=b, g=g: e.tensor_tensor(out=yb[:, g, :], in0=ps[:, b, :], in1=gu[:, g, :], op=ALU.mult),
                           reads=[bkR[b], guR[g]], writes=[ybR[g]])
                    after_group()

                def conformer_ln():
                    P.flush()
                    b1, b2 = nb(), nb()
                    for kc in range(KC):
                        t, tR = newtmp()
                        tb1 = t[:, 0:T // 2].bitcast(BF16)
                        tb2 = t[:, T // 2:T].bitcast(BF16)
                        P.emit(act, lambda e, tb1=tb1, kc=kc: e.activation(out=tb1, in_=cacc[:, kc, :], func=AF.Copy), reads=[caR[kc]], writes=[tR])
                        P.emit(act, lambda e, tb2=tb2, kc=kc: e.activation(out=tb2, in_=cacc[:, kc, :], func=AF.Square), reads=[caR[kc]], writes=[tR])
                        P.emit(pe, lambda e, tb1=tb1, kc=kc: e.matmul(ps[:, b1, :], ones_m[:, :], tb1, start=(kc == 0), stop=(kc == KC - 1)),
                               reads=[tR, onesR], writes=[bkR[b1]] if kc == 0 else [], inc=True)
                        P.emit(pe, lambda e, tb2=tb2, kc=kc: e.matmul(ps[:, b2, :], ones_m[:, :], tb2, start=(kc == 0), stop=(kc == KC - 1)),
                               reads=[tR, onesR], writes=[bkR[b2]] if kc == 0 else [], inc=True)
                    bkR[b1].w = (pe.key, pe.cnt - 1)
                    group_done(b2)
                    warm(WARM)
                    P.emit(act, lambda e: e.activation(out=dmy[:, 1:2], in_=epst[:, 0:1], func=AF.Ln), reads=[onesR], writes=[dmyR])
                    P.emit(act, lambda e: e.activation(out=mean_t[:, :], in_=ps[:, b1, :], func=AF.Copy), reads=[bkR[b1]], writes=[meanR])
                    t, tR = newtmp()
                    P.emit(dve, lambda e, t=t: e.tensor_tensor(out=t[:, 0:T], in0=mean_t[:, :], in1=mean_t[:, :], op=ALU.mult), reads=[meanR], writes=[tR])
                    P.emit(dve, lambda e, t=t: e.tensor_tensor(out=t[:, 0:T], in0=ps[:, b2, :], in1=t[:, 0:T], op=ALU.subtract), reads=[bkR[b2], tR], writes=[tR])
                    rC, rCR = rs[1], rsR[1]
                    P.emit(act, lambda e, t=t: e.activation(out=rC[:, :], in_=t[:, 0:T], func=AF.Ln, bias=epst[:, 0:1], scale=1.0), reads=[tR, onesR], writes=[rCR])
                    P.emit(act, lambda e: e.activation(out=rC[:, :], in_=rC[:, :], func=AF.Exp, scale=-0.5), reads=[rCR], writes=[rCR])
                    for kc in range(KC):
                        t, tR = newtmp()
                        eng = dve
                        P.emit(eng, lambda e, t=t, kc=kc: e.tensor_tensor(out=t[:, 0:T], in0=cacc[:, kc, :], in1=mean_t[:, :], op=ALU.subtract),
                               reads=[caR[kc], meanR], writes=[tR])
                        P.emit(eng, lambda e, t=t: e.tensor_tensor(out=t[:, 0:T], in0=t[:, 0:T], in1=rC[:, :], op=ALU.mult), reads=[tR, rCR], writes=[tR])
                        P.emit(act, lambda e, t=t, kc=kc: e.activation(out=yc[:, kc, :], in_=t[:, 0:T], func=AF.Silu,
                                                                       scale=cvcol(l, CV_CLG, kc), bias=cvcol(l, CV_CLB, kc)),
                               reads=[tR, cvR], writes=[ycR[kc]])

                for X, (yX, yXR) in enumerate(((ya, yaR), (yb, ybR), (yc, ycR))):
                    if X == 2:
                        conformer_ln()
                    for half in range(2):
                        wg, wgR = next_piece("w", l, P_G + X * 4 + half * 2)
                        wo, woR = next_piece("w", l, P_G + X * 4 + half * 2 + 1)
                        pre = {}
                        if X == 2 and half == 0:
                            for j in range(4):
                                bg_ = nb()
                                group(bg_, ps[:, bg_, :], [(wg[:, kc, j * 128:(j + 1) * 128], hb[:, kc, :], [wgR, hR[kc]]) for kc in range(KC)])
                                group_done(bg_)
                                sg, sgR = newtmp()
                                P.emit(act, lambda e, sg=sg, bg_=bg_: e.activation(out=sg[:, 0:T], in_=ps[:, bg_, :], func=AF.Sigmoid), reads=[bkR[bg_]], writes=[sgR])
                                pre[j] = (sg, sgR)
                        ybk = {}
                        if pre:
                            specs = []
                            for j in range(4):
                                ybk[j] = nb()
                                specs.append((ybk[j], ps[:, ybk[j], :], [(wo[:, kc, j * 128:(j + 1) * 128], yX[:, kc, :], [woR, yXR[kc]]) for kc in range(KC)]))
                            groups_ilv(specs)
                        for j in range(4):
                            oc = half * 4 + j
                            if j in pre:
                                sg, sgR = pre[j]
                                by_ = ybk[j]
                            else:
                                bg_, by_ = nb(), nb()
                                group(bg_, ps[:, bg_, :], [(wg[:, kc, j * 128:(j + 1) * 128], hb[:, kc, :], [wgR, hR[kc]]) for kc in range(KC)])
                                group_done(bg_)
                            if j not in pre:
                                group(by_, ps[:, by_, :], [(wo[:, kc, j * 128:(j + 1) * 128], yX[:, kc, :], [woR, yXR[kc]]) for kc in range(KC)])
                                group_done(by_)
                            if j not in pre:
                                sg, sgR = newtmp()
                                P.emit(act, lambda e, sg=sg, bg_=bg_: e.activation(out=sg[:, 0:T], in_=ps[:, bg_, :], func=AF.Sigmoid), reads=[bkR[bg_]], writes=[sgR])
                            if X == 0:
                                P.emit(dve, lambda e, sg=sg, by_=by_, oc=oc: e.tensor_tensor(out=mm_[:, oc, :], in0=ps[:, by_, :], in1=sg[:, 0:T], op=ALU.mult),
                                       reads=[bkR[by_], sgR], writes=[mR[oc]])
                            else:
                                P.emit(dve, lambda e, sg=sg, by_=by_: e.tensor_tensor(out=sg[:, 0:T], in0=ps[:, by_, :], in1=sg[:, 0:T], op=ALU.mult),
                                       reads=[bkR[by_], sgR], writes=[sgR])
                                if X == 1:
                                    P.emit(pool, lambda e, sg=sg, oc=oc: e.tensor_tensor(out=mm_[:, oc, :], in0=mm_[:, oc, :], in1=sg[:, 0:T], op=ALU.add),
                                           reads=[mR[oc], sgR], writes=[mR[oc]])
                                else:
                                    P.emit(pool, lambda e, sg=sg, oc=oc: e.tensor_tensor(out=mbf[:, oc, :], in0=mm_[:, oc, :], in1=sg[:, 0:T], op=ALU.add),
                                           reads=[mR[oc], sgR], writes=[mbR[oc]])
                            if X < 2:
                                P.pump("dve", 2 * PUMP_P)

                ada_hook("after_G")
                if l == NL - 1 and it + 1 < NT:
                    P.emit(sp, lambda e: e.dma_start(out=XB[1 - cur][:, :, :], in_=x_d[:, :, t0 + T:t0 + 2 * T]), writes=XS_R[1 - cur], dsem=xS2[1 - cur])

                for half in range(2):
                    wo, woR = next_piece("w", l, P_O + half)
                    for j in range(4):
                        oc = half * 4 + j
                        b = nb()
                        group(b, ps[:, b, :], [(wo[:, kc, j * 128:(j + 1) * 128], mbf[:, kc, :], [woR, mbR[kc]]) for kc in range(KC)])
                        group_done(b)
                        P.emit(dve, lambda e, b=b, oc=oc: e.scalar_tensor_tensor(out=xs[:, oc, :], in0=ps[:, b, :], scalar=g1v[:, oc:oc + 1], in1=xs[:, oc, :],
                                                                                 op0=ALU.mult, op1=ALU.add),
                               reads=[bkR[b], modR[l][4], modR[l][5], xsR[oc]], writes=[xsR[oc]])

                rmsnorm_to(l, At[:, l, 8:16], sh2, hb, hR, xreads=[AR2[l], modR[l][6], modR[l][7]])
                for i in range(11):
                    wt, wr_ = next_piece("w", l, P_FI + i)
                    specs, fbk = [], []
                    for jj in range(2):
                        bg_, bu_ = nb(), nb()
                        fbk.append((bg_, bu_))
                        specs.append((bg_, ps[:, bg_, :], [(wt[:, kc, (2 * jj) * 128:(2 * jj + 1) * 128], hb[:, kc, :], [wr_, hR[kc]]) for kc in range(KC)]))
                        specs.append((bu_, ps[:, bu_, :], [(wt[:, kc, (2 * jj + 1) * 128:(2 * jj + 2) * 128], hb[:, kc, :], [wr_, hR[kc]]) for kc in range(KC)]))
                    (groups_ilv if i == 0 else groups_seq)(specs)
                    for jj in range(2):
                        j = 2 * i + jj
                        bg_, bu_ = fbk[jj]
                        sg, sgR = newtmp()
                        P.emit(act, lambda e, sg=sg, bg_=bg_: e.activation(out=sg[:, 0:T], in_=ps[:, bg_, :], func=AF.Silu), reads=[bkR[bg_]], writes=[sgR])
                        P.emit(dve, lambda e, sg=sg, bu_=bu_, j=j: e.tensor_tensor(out=ff[:, j, :], in0=ps[:, bu_, :], in1=sg[:, 0:T], op=ALU.mult),
                               reads=[bkR[bu_], sgR], writes=[ffR[j]])
                    ada_hook("ffn_in_%d" % i)
                if l == NL - 1 and it + 1 < NT:
                    rmsnorm_to(0, At[:, 0, 0:8], modt[:, 0, 0:8], hb, hR, xreads=[AR1[0], modR[0][0], modR[0][1]],
                               xs_=XB[1 - cur], xsR_=XS_R[1 - cur])
                for ch in range(2):
                    banks = [nb() for _ in range(4)]
                    for kp in range(3):
                        wt, wr_ = next_piece("w", l, P_FO + ch * 3 + kp)
                        nk = 8 if kp < 2 else 6
                        for j in range(4):
                            for kcl in range(nk):
                                kc = kp * 8 + kcl
                                b = banks[j]
                                P.emit(pe, lambda e, wt=wt, kcl=kcl, kc=kc, j=j, b=b: e.matmul(ps[:, b, :], wt[:, kcl, j * 128:(j + 1) * 128], ff[:, kc, :],
                                                                                               start=(kc == 0), stop=(kc == FKC - 1)),
                                       reads=[wr_, ffR[kc]], writes=[bkR[b]] if kc == 0 else [], inc=(kcl == nk - 1))
                        if ch == 0 and kp == 0:
                            ada_hook("ffn_out_0")
                    for j in range(4):
                        oc = ch * 4 + j
                        b = banks[j]
                        bkR[b].w = (pe.key, pe.cnt)
                        P.emit(dve, lambda e, b=b, oc=oc: e.scalar_tensor_tensor(out=xs[:, oc, :], in0=ps[:, b, :], scalar=g2v[:, oc:oc + 1], in1=xs[:, oc, :],
                                                                                 op0=ALU.mult, op1=ALU.add),
                               reads=[bkR[b], modR[l][10], modR[l][11], xsR[oc]], writes=[xsR[oc]])

            def final_norm(xb=XB[cur], xbR=XS_R[cur], t0=t0):
                rmsnorm_to(0, fgt, None, ost2, osR2, xs_=xb, xsR_=xbR, ri=1)
                P.emit(sp, lambda e: e.dma_start(out=y_d[:, :, t0:t0 + T], in_=ost2), reads=osR2, dsem=oS)
            pending_final.append(final_norm)
        while pending_final:
            pending_final.pop(0)()
        sp.be.wait_ge(oS.sem, oS.cnt)
    return nc


_OFF = dict(b_a=0, c_a=1024, x_a=2048, u=3072, v=4096, a_c=5120, g_c=6144, gate_a=7168, gate_b=8192, gate_c=9216)


def _pack(Wm, cols):
    sub = Wm[:, cols]
    return np.ascontiguousarray(sub.reshape(KC, 128, 512).transpose(1, 0, 2)).reshape(128, 4096)


def _sl(name, oc):
    return np.arange(_OFF[name] + oc * 128, _OFF[name] + (oc + 1) * 128)


def _pack_layer(w_in, w_a, w_b, w_c, w_o, w_fi, w_fo, conv_w):
    slots = []
    for oc in range(8):
        slots += [_sl("a_c", oc), _sl("g_c", oc)]
    for oc in range(8):
        slots.append(_sl("u", oc))
    for oc in range(8):
        slots.append(_sl("v", oc))
    for oc in range(8):
        slots += [_sl("c_a", oc), _sl("x_a", oc), _sl("b_a", oc)]
    pieces = [_pack(w_in, np.concatenate(slots[4 * i:4 * i + 4])) for i in range(14)]
    for X, Wout in (("gate_a", w_a), ("gate_b", w_b), ("gate_c", w_c)):
        for half in range(2):
            pieces.append(_pack(w_in, np.concatenate([_sl(X, half * 4 + j) for j in range(4)])))
            pieces.append(_pack(Wout, np.arange(half * 512, half * 512 + 512)))
    for half in range(2):
        pieces.append(_pack(w_o, np.arange(half * 512, half * 512 + 512)))
    for i in range(11):
        cols = []
        for j in (2 * i, 2 * i + 1):
            cols += [np.arange(j * 128, (j + 1) * 128), DFF + np.arange(j * 128, (j + 1) * 128)]
        pieces.append(_pack(w_fi, np.concatenate(cols)))
    for ch in range(2):
        for kp in range(3):
            nk = 8 if kp < 2 else 6
            blk = np.zeros((128, KC, 512), np.float32)
            rows = w_fo[kp * 1024:kp * 1024 + nk * 128, ch * 512:(ch + 1) * 512]
            blk[:, :nk, :] = rows.reshape(nk, 128, 512).transpose(1, 0, 2)
            pieces.append(blk.reshape(128, 4096))
    for i in range(NCP):
        oc = NDV + i
        blk = np.zeros((128, 32, 128), np.float32)
        idx = np.arange(128)
        for k in range(31):
            blk[idx, k, idx] = conv_w[k, oc * 128:(oc + 1) * 128]
        pieces.append(blk.reshape(128, 4096))
    assert len(pieces) == NPC
    return np.stack(pieces)


def _pcol(v):
    v = np.asarray(v)
    lead = v.shape[:-1]
    n = v.shape[-1] // 128
    r = v.reshape(lead + (n, 128))
    r = np.moveaxis(r, -1, 0)
    return r.reshape(128, -1)


_CACHE = {}
_LOG = None


def _host_prep(inputs, NL):
    f32 = np.float32
    wpk = np.stack([_pack_layer(inputs["w_in"][l], inputs["w_a_out"][l], inputs["w_b_out"][l], inputs["w_c_out"][l],
                                inputs["w_o"][l], inputs["w_ffn_in"][l], inputs["w_ffn_out"][l], inputs["cfm_conv_w"][l]) for l in range(NL)]).astype(f32)
    wada = np.stack([np.stack([_pack(inputs["w_ada"][l], np.arange(j * 512, (j + 1) * 512)) for j in range(NADA)]) for l in range(NL)]).astype(f32)
    cv = np.zeros((NL, 128, NCV), f32)
    for l in range(NL):
        cv[l, :, CV_N1G:CV_N1G + 8] = _pcol(inputs["norm1_g"][l])
        cv[l, :, CV_N2G:CV_N2G + 8] = _pcol(inputs["norm2_g"][l])
        cv[l, :, CV_WSH:CV_WSH + 24] = _pcol(inputs["w_short"][l])
        cv[l, :, CV_CW:CV_CW + 248] = _pcol(inputs["cfm_conv_w"][l])
        cv[l, :, CV_CB:CV_CB + 8] = _pcol(inputs["cfm_conv_b"][l])
        cv[l, :, CV_CLG:CV_CLG + 8] = _pcol(inputs["cfm_ln_g"][l])
        cv[l, :, CV_CLB:CV_CLB + 8] = _pcol(inputs["cfm_ln_b"][l])
        cv[l, :, CV_BADA:CV_BADA + 48] = _pcol(inputs["b_ada"][l])
    fg = _pcol(inputs["final_g"]).astype(f32)
    sln = np.stack([np.concatenate([np.broadcast_to(inputs["sgu_ln_g"][l][None, :], (128, 1024)),
                                    np.broadcast_to(inputs["sgu_ln_b"][l][None, :], (128, 1024))], axis=1) for l in range(NL)]).astype(f32)
    wsg = np.ascontiguousarray(np.transpose(inputs["w_sgu"][:NL], (0, 3, 1, 2))).reshape(NL, 128, 1024).astype(f32)
    s_i = np.arange(128)[:, None]
    t_i = np.arange(128)[None, :]
    msk = np.tile((s_i <= t_i).astype(f32), (1, 8))
    bsg = np.ascontiguousarray(inputs["b_sgu"][:NL]).reshape(NL, 1, 1024).astype(f32)
    return dict(wpk=wpk, wada=wada, cv=cv, fg=fg, sln=np.ascontiguousarray(sln), wsg=wsg, msk=msk, bsg=bsg)


def kernel(**inputs):
    inputs = {k: np.asarray(v) for k, v in inputs.items()}
    x = inputs["x"]
    B, S, _ = x.shape
    NL = inputs["w_in"].shape[0]
    key = (S, NL)
    if key not in _CACHE:
        _CACHE[key] = build_program(S, NL)
    nc = _CACHE[key]
    shared = _host_prep(inputs, NL)
    in_maps = []
    for b in range(B):
        m = dict(shared)
        m["x"] = np.ascontiguousarray(x[b].T.reshape(KC, 128, S).transpose(1, 0, 2)).astype(np.float32)
        m["cc"] = _pcol(inputs["c"][b]).astype(np.float32)
        in_maps.append(m)
    res = run_bass_kernel_spmd(nc, in_maps, core_ids=list(range(B)))
    out = np.empty((B, S, D), np.float32)
    for b in range(B):
        y = np.asarray(res.results[b]["y"])
        out[b] = y.transpose(2, 1, 0).reshape(S, D)
    return out
```
